# Optimizing a Trainium2 kernel written in Bass

```python
import math
import jax, jax.numpy as jnp
from jax import lax
import numpy as np

D_MODEL = 1024
BATCH = 2
SEQ = 8192
DEPTH = 2

CHUNK = 64
Q_BLOCK = 128
N_A_LAYERS = DEPTH // 2
N_B_LAYERS = DEPTH - N_A_LAYERS
DIFF_HEADS = 8
DIFF_HEAD_DIM = D_MODEL // (2 * DIFF_HEADS)
DIFF_V_DIM = 2 * DIFF_HEAD_DIM
SB_HEADS = 16
SB_HEAD_DIM = D_MODEL // SB_HEADS
PEER_HEADS = 8
PEER_N_KEYS = 128
PEER_N_EXPERTS = PEER_N_KEYS * PEER_N_KEYS
PEER_TOPK = 16
PEER_QUERY_DIM = 128
PEER_HALF = PEER_QUERY_DIM // 2
PEER_TOKEN_BLOCK = 128
ROPE_THETA = 10000.0
LN_EPS = 1e-5
RMS_EPS = 1e-5
DEEPNORM_ALPHA = (2.0 * DEPTH) ** 0.25
DEEPNORM_BETA = (8.0 * DEPTH) ** -0.25

kernel_name = "yoco_diffattn_stickbreak_peer_deepnorm"


def _layer_norm(x, g, b):
    xf = x.astype(jnp.float32)
    mu = jnp.mean(xf, axis=-1, keepdims=True)
    var = jnp.mean(jnp.square(xf - mu), axis=-1, keepdims=True)
    y = (xf - mu) * lax.rsqrt(var + LN_EPS)
    return (y * g.astype(jnp.float32) + b.astype(jnp.float32)).astype(x.dtype)


def _rope_tables(seq):
    pos = jnp.arange(seq, dtype=jnp.float32)
    inv = ROPE_THETA ** (-jnp.arange(0, DIFF_HEAD_DIM, 2, dtype=jnp.float32) / DIFF_HEAD_DIM)
    ang = pos[:, None] * inv[None, :]
    ang = jnp.concatenate([ang, ang], axis=-1)
    return jnp.cos(ang), jnp.sin(ang)


def _rope(t, cos, sin):
    half = t.shape[-1] // 2
    rot = jnp.concatenate([-t[..., half:], t[..., :half]], axis=-1)
    return (t * cos[None, :, None, :] + rot * sin[None, :, None, :]).astype(t.dtype)


def _diff_attention(x, w_qkv, w_o, lam_params, subln_g, lambda_init, cos, sin):
    B, S, D = x.shape
    nb = S // Q_BLOCK
    qkv = x @ w_qkv
    q, k, v = qkv[..., :D], qkv[..., D:2 * D], qkv[..., 2 * D:]
    q = _rope(q.reshape(B, S, 2 * DIFF_HEADS, DIFF_HEAD_DIM), cos, sin)
    k = _rope(k.reshape(B, S, 2 * DIFF_HEADS, DIFF_HEAD_DIM), cos, sin)
    v = v.reshape(B, S, DIFF_HEADS, DIFF_V_DIM)
    kh = k.reshape(B, S, DIFF_HEADS, 2, DIFF_HEAD_DIM).transpose(0, 2, 3, 1, 4)
    vh = v.transpose(0, 2, 1, 3)
    qb = q.reshape(B, nb, Q_BLOCK, DIFF_HEADS, 2, DIFF_HEAD_DIM).transpose(1, 0, 3, 4, 2, 5)
    lp = lam_params.astype(jnp.float32)
    lam = jnp.exp(jnp.sum(lp[0] * lp[1])) - jnp.exp(jnp.sum(lp[2] * lp[3])) + lambda_init
    key_chunk = jnp.arange(S) // CHUNK
    scale = DIFF_HEAD_DIM ** -0.5

    def block(args):
        qblk, i = args
        t = i * Q_BLOCK + jnp.arange(Q_BLOCK)
        mask = key_chunk[None, :] <= (t // CHUNK)[:, None]
        s = jnp.einsum('bhmqd,bhmkd->bhmqk', qblk, kh).astype(jnp.float32) * scale
        p = jax.nn.softmax(jnp.where(mask, s, -jnp.inf), axis=-1)
        a = p[:, :, 0] - lam * p[:, :, 1]
        return jnp.einsum('bhqk,bhkd->bhqd', a.astype(vh.dtype), vh)

    o = lax.map(block, (qb, jnp.arange(nb)))
    of = o.astype(jnp.float32)
    of = of * lax.rsqrt(jnp.mean(of * of, axis=-1, keepdims=True) + RMS_EPS)
    of = of * subln_g.astype(jnp.float32) * (1.0 - lambda_init)
    o = of.astype(x.dtype).transpose(1, 0, 3, 2, 4).reshape(B, S, DIFF_HEADS * DIFF_V_DIM)
    return o @ w_o


def _shared_kv(x, w_kv):
    B, S, D = x.shape
    kv = x @ w_kv
    k = kv[..., :D].reshape(B, S, SB_HEADS, SB_HEAD_DIM).transpose(0, 2, 1, 3)
    v = kv[..., D:].reshape(B, S, SB_HEADS, SB_HEAD_DIM).transpose(0, 2, 1, 3)
    return k, v


def _stick_breaking(x, w_q, w_o, k_sb, v_sb):
    B, S, D = x.shape
    nb = S // Q_BLOCK
    q = (x @ w_q).reshape(B, nb, Q_BLOCK, SB_HEADS, SB_HEAD_DIM).transpose(1, 0, 3, 2, 4)
    key_pos = jnp.arange(S)
    scale = SB_HEAD_DIM ** -0.5

    def block(args):
        qblk, i = args
        t = i * Q_BLOCK + jnp.arange(Q_BLOCK)
        mask = key_pos[None, :] < t[:, None]
        z = jnp.einsum('bhqd,bhkd->bhqk', qblk, k_sb).astype(jnp.float32) * scale
        log_one_minus = jnp.where(mask, jax.nn.log_sigmoid(-z), 0.0)
        tail = lax.cumsum(log_one_minus, axis=3, reverse=True) - log_one_minus
        a = jnp.where(mask, jnp.exp(jax.nn.log_sigmoid(z) + tail), 0.0)
        return jnp.einsum('bhqk,bhkd->bhqd', a.astype(v_sb.dtype), v_sb)

    o = lax.map(block, (q, jnp.arange(nb)))
    o = o.transpose(1, 0, 3, 2, 4).reshape(B, S, SB_HEADS * SB_HEAD_DIM)
    return o @ w_o


def _peer(x, w_pq, sub_keys, u, v):
    B, S, D = x.shape
    T = B * S
    xt = x.reshape(T, D)
    q = (xt @ w_pq).reshape(T, PEER_HEADS, 2, PEER_HALF)
    scores = jnp.einsum('thcd,hckd->thck', q, sub_keys).astype(jnp.float32)
    s_top, i_top = lax.top_k(scores, PEER_TOPK)
    cand = s_top[:, :, 0, :, None] + s_top[:, :, 1, None, :]
    cand_ids = i_top[:, :, 0, :, None] * PEER_N_KEYS + i_top[:, :, 1, None, :]
    g_s, g_i = lax.top_k(cand.reshape(T, PEER_HEADS, PEER_TOPK * PEER_TOPK), PEER_TOPK)
    expert_ids = jnp.take_along_axis(cand_ids.reshape(T, PEER_HEADS, PEER_TOPK * PEER_TOPK), g_i, axis=-1)
    gates = jax.nn.softmax(g_s, axis=-1).astype(x.dtype)
    nblk = T // PEER_TOKEN_BLOCK
    n_sel = PEER_HEADS * PEER_TOPK
    ids_b = expert_ids.reshape(nblk, PEER_TOKEN_BLOCK, n_sel)
    g_b = gates.reshape(nblk, PEER_TOKEN_BLOCK, n_sel)
    x_b = xt.reshape(nblk, PEER_TOKEN_BLOCK, D)

    def block(args):
        xblk, idb, gb = args
        u_sel = jnp.take(u, idb, axis=0)
        h = jax.nn.gelu(jnp.einsum('td,ted->te', xblk, u_sel), approximate=False)
        v_sel = jnp.take(v, idb, axis=0)
        return jnp.einsum('te,ted->td', gb * h, v_sel)

    out = lax.map(block, (x_b, ids_b, g_b))
    return out.reshape(B, S, D)


def setup_inputs(seed: int = 0) -> dict:
    key = jax.random.key(seed)
    ks = jax.random.split(key, 16)
    D = D_MODEL
    f32 = jnp.float32
    nrm = lambda k, shape, s: jax.random.normal(k, shape, f32) * s
    v_col_scale = jnp.concatenate([jnp.ones((2 * D,), f32), jnp.full((D,), DEEPNORM_BETA, f32)])
    kv_col_scale = jnp.concatenate([jnp.ones((D,), f32), jnp.full((D,), DEEPNORM_BETA, f32)])
    return {
        "x": nrm(ks[0], (BATCH, SEQ, D), 1.0),
        "ln_g": 1.0 + nrm(ks[1], (DEPTH, 2, D), 0.02),
        "ln_b": nrm(ks[2], (DEPTH, 2, D), 0.02),
        "w_qkv_a": nrm(ks[3], (N_A_LAYERS, D, 3 * D), D ** -0.5) * v_col_scale,
        "w_o_a": nrm(ks[4], (N_A_LAYERS, DIFF_HEADS * DIFF_V_DIM, D), D ** -0.5 * DEEPNORM_BETA),
        "lambda_qk_a": nrm(ks[5], (N_A_LAYERS, 4, DIFF_HEAD_DIM), 0.1),
        "subln_g_a": 1.0 + nrm(ks[6], (N_A_LAYERS, DIFF_V_DIM), 0.02),
        "w_kv_b": nrm(ks[7], (D, 2 * D), D ** -0.5) * kv_col_scale,
        "w_q_b": nrm(ks[8], (N_B_LAYERS, D, SB_HEADS * SB_HEAD_DIM), D ** -0.5),
        "w_o_b": nrm(ks[9], (N_B_LAYERS, SB_HEADS * SB_HEAD_DIM, D), D ** -0.5 * DEEPNORM_BETA),
        "peer_w_q": nrm(ks[10], (DEPTH, D, PEER_HEADS * PEER_QUERY_DIM), D ** -0.5),
        "peer_sub_keys": nrm(ks[11], (DEPTH, PEER_HEADS, 2, PEER_N_KEYS, PEER_HALF), PEER_HALF ** -0.5),
        "peer_u": nrm(ks[12], (DEPTH, PEER_N_EXPERTS, D), D ** -0.5),
        "peer_v": nrm(ks[13], (DEPTH, PEER_N_EXPERTS, D), DEEPNORM_BETA * PEER_HEADS ** -0.5),
    }


def reference(x, ln_g, ln_b, w_qkv_a, w_o_a, lambda_qk_a, subln_g_a, w_kv_b, w_q_b, w_o_b,
              peer_w_q, peer_sub_keys, peer_u, peer_v):
    S = x.shape[1]
    cos, sin = _rope_tables(S)
    k_sb = None
    v_sb = None
    for layer in range(DEPTH):
        if layer < N_A_LAYERS:
            lambda_init = 0.8 - 0.6 * math.exp(-0.3 * layer)
            mix = _diff_attention(x, w_qkv_a[layer], w_o_a[layer], lambda_qk_a[layer],
                                  subln_g_a[layer], lambda_init, cos, sin)
        else:
            if layer == N_A_LAYERS:
                k_sb, v_sb = _shared_kv(x, w_kv_b)
            j = layer - N_A_LAYERS
            mix = _stick_breaking(x, w_q_b[j], w_o_b[j], k_sb, v_sb)
        x = _layer_norm(DEEPNORM_ALPHA * x + mix, ln_g[layer, 0], ln_b[layer, 0])
        ffn = _peer(x, peer_w_q[layer], peer_sub_keys[layer], peer_u[layer], peer_v[layer])
        x = _layer_norm(DEEPNORM_ALPHA * x + ffn, ln_g[layer, 1], ln_b[layer, 1])
    return x
```

```python
import math
from contextlib import ExitStack

import numpy as np
import ml_dtypes
import concourse.bass as bass
import concourse.mybir as mybir
from concourse.bass_utils import run_bass_kernel_spmd

F32 = mybir.dt.float32
BF16 = mybir.dt.bfloat16
I32 = mybir.dt.int32
AF = mybir.ActivationFunctionType
ALU = mybir.AluOpType
NDS = 40

D = 1024
S = 8192
NB = 16
TQ = 2048
ALPHA = 4.0 ** 0.25
LN_EPS = 1e-5
NEXP = 16384
TWO_PI_SAFE = 6.283185
_DBG = {}


class Prog:
    def __init__(self, nc, es):
        self.nc = nc
        self.es = es
        self.engs = {'pe': nc.tensor, 'act': nc.scalar, 'dve': nc.vector,
                     'pool': nc.gpsimd, 'sp': nc.sync}
        self.sem = {}
        self.cnt = {}
        for e in ['pe', 'act', 'dve', 'pool']:
            self.sem[e] = es.enter_context(nc.semaphore('s_' + e))
            self.cnt[e] = 0
        for i in range(NDS):
            self.sem['d%d' % i] = es.enter_context(nc.semaphore('sd%d' % i))
        self.dcnt = [0] * NDS
        self.dnext = 0
        self.seen = {e: {} for e in self.engs}
        self.bufs = {}
        self.know = {}
        self.nwaits = 0
        self.nops = 0

    def sb(self, name, shape, dt):
        self.uid = getattr(self, 'uid', 0) + 1
        return self.es.enter_context(self.nc.sbuf_tensor('sb%d_%s' % (self.uid, name), shape, dt))

    def ps(self, name, shape, dt):
        return self.es.enter_context(self.nc.psum_tensor(name, shape, dt))

    def _deps(self, reads, writes):
        deps = {}

        def add(k, v):
            if deps.get(k, 0) < v:
                deps[k] = v
        for b in reads:
            st = self.bufs.get(b)
            if st and st['w']:
                add(*st['w'])
        for b in writes:
            st = self.bufs.get(b)
            if st:
                if st['w']:
                    add(*st['w'])
                for k, v in st['r'].items():
                    add(k, v)
        return deps

    CE = ('pe', 'act', 'dve', 'pool')

    def _wait(self, e, deps):
        se = self.seen[e]
        for k, v in deps.items():
            if k == e and e == 'pe':
                continue
            if se.get(k, 0) < v:
                self.engs[e].wait_ge(self.sem[k], v)
                se[k] = v
                self.nwaits += 1
            kn = self.know.get((k, v))
            if kn is not None:
                for c, cv in zip(self.CE, kn):
                    if se.get(c, 0) < cv:
                        se[c] = cv

    def _snap(self, e, ev):
        se = self.seen[e]
        self.know[ev] = tuple(se.get(c, 0) for c in self.CE)

    def _record(self, ev, reads, writes):
        k, v = ev
        for b in reads:
            st = self.bufs.setdefault(b, {'w': None, 'r': {}})
            if st['r'].get(k, 0) < v:
                st['r'][k] = v
        for b in writes:
            self.bufs[b] = {'w': ev, 'r': {}}

    def op(self, e, fn, reads=(), writes=()):
        self._wait(e, self._deps(reads, writes))
        inst = fn(self.engs[e])
        self.cnt[e] += 1
        inst.then_inc(self.sem[e], 1)
        ev = (e, self.cnt[e])
        self._snap(e, ev)
        self._record(ev, reads, writes)
        self.nops += 1
        return ev

    def dma(self, q, out, in_, reads=(), writes=(), **kw):
        deps = self._deps(reads, writes)
        slot = self.dnext
        self.dnext = (slot + 1) % NDS
        k = 'd%d' % slot
        if self.dcnt[slot] > 0:
            deps[k] = max(deps.get(k, 0), 16 * self.dcnt[slot])
        self._wait(q, deps)
        inst = self.engs[q].dma_start(out=out, in_=in_, **kw)
        self.dcnt[slot] += 1
        inst.then_inc(self.sem[k], 16)
        ev = (k, 16 * self.dcnt[slot])
        self._snap(q, ev)
        self._record(ev, reads, writes)
        return ev

    def collective(self, kind, replica_groups, in_ap, out_ap, reads=(), writes=()):
        deps = self._deps(reads, writes)
        self._wait('pool', deps)
        self.ncc = getattr(self, 'ncc', 0) + 1
        k = 'cc%d' % self.ncc
        self.sem[k] = self.es.enter_context(self.nc.semaphore('s_' + k))
        inst = self.nc.gpsimd.collective_compute(kind, ALU.bypass, replica_groups=replica_groups,
                                                 ins=[in_ap.opt()], outs=[out_ap.opt()])
        inst.then_inc(self.sem[k], 1)
        ev = (k, 1)
        self._record(ev, reads, writes)
        return ev

    def wait_all(self, e, bufs):
        self._wait(e, self._deps(bufs, ()))

    def barrier(self):
        deps = {e: self.cnt[e] for e in ['pe', 'act', 'dve', 'pool'] if self.cnt[e] > 0}
        for i in range(NDS):
            if self.dcnt[i] > 0:
                deps['d%d' % i] = 16 * self.dcnt[i]
        for i in range(getattr(self, 'ncc', 0)):
            deps['cc%d' % (i + 1)] = 1
        for e in self.engs:
            self._wait(e, dict(deps))


class St:
    pass


def setup_common(p, st):
    st.PS = [p.ps('psb%d' % i, [128, 1024], F32) for i in range(4)]
    st.XT = p.sb('XT', [128, 8, TQ], BF16)
    st.identf = p.sb('identf', [128, 128], F32)
    st.ident = p.sb('ident', [128, 128], BF16)
    p.op('pool', lambda e: e.memset(st.identf[:], 1.0), writes=['identf'])
    p.op('pool', lambda e: e.affine_select(out=st.identf[:], in_=st.identf[:], pattern=[[-1, 128]],
                                           compare_op=ALU.is_equal, fill=0.0, base=0, channel_multiplier=1),
         reads=['identf'], writes=['identf'])
    p.op('dve', lambda e: e.tensor_copy(out=st.ident[:], in_=st.identf[:]), reads=['identf'], writes=['ident'])
    st.eps = p.sb('epsc', [128, 1], F32)
    p.op('pool', lambda e: e.memset(st.eps[:], LN_EPS), writes=['eps'])
    st.one = p.sb('onec', [128, 1], F32)
    p.op('pool', lambda e: e.memset(st.one[:], 1.0), writes=['one'])
    st.lnw = p.sb('lnw', [128, 2, D], F32)
    st.ysb = p.sb('ysb', [128, D], F32)
    st.xin = [p.sb('xin%d' % i, [128, D], F32) for i in range(2)]
    st.xo = [p.sb('xo%d' % i, [128, D], F32) for i in range(2)]
    st.lnst = p.sb('lnst', [128, 2, 6], F32)
    st.lnmv = p.sb('lnmv', [128, 4], F32)


def bank(st, i):
    return st.PS[i // 2][:, (i % 2) * 512:(i % 2) * 512 + 512]


def xt_from_tile(p, st, blk, xt_tile, xt_key):
    for half in range(2):
        bk = 6 + half
        pst = bank(st, bk)
        for i in range(4):
            dc = half * 4 + i
            p.op('pe', lambda e: e.transpose(out=pst[:, i * 128:(i + 1) * 128], in_=xt_tile[:, dc * 128:(dc + 1) * 128],
                                             identity=st.identf[:]),
                 reads=[xt_key, 'identf'], writes=[('bank', bk)])
        outap = st.XT[:, half * 4:half * 4 + 4, blk * 128:(blk + 1) * 128]
        inap = pst.rearrange("p (a b) -> p a b", a=4)
        if half == 0:
            p.op('act', lambda e: e.copy(out=outap, in_=inap), reads=[('bank', bk)], writes=[('XT', blk, half)])
        else:
            p.op('dve', lambda e: e.tensor_copy(out=outap, in_=inap), reads=[('bank', bk)], writes=[('XT', blk, half)])


def make_xt(p, st, x_dram, xkey):
    for blk in range(NB):
        t = st.xin[blk % 2]
        p.dma('sp', t[:], x_dram[blk * 128:(blk + 1) * 128, :], reads=[(xkey, blk)], writes=[('xin', blk % 2)])
        xt_from_tile(p, st, blk, t, ('xin', blk % 2))


def xt_keys(blks):
    return [('XT', b, h) for b in blks for h in range(2)]


def load_ln(p, st, g_dram, b_dram):
    p.dma('sp', st.lnw[:, 0, :], g_dram[0:1, :].to_broadcast([128, D]), writes=['lng'])
    p.dma('sp', st.lnw[:, 1, :], b_dram[0:1, :].to_broadcast([128, D]), writes=['lnb'])


def finish_block(p, st, blk, ps_tile, ps_keys, x_src, skey, x_dst, dkey, do_xt=True):
    xi = st.xin[blk % 2]
    xo = st.xo[blk % 2]
    p.dma('sp', xi[:], x_src[blk * 128:(blk + 1) * 128, :], reads=[(skey, blk)], writes=[('xin', blk % 2)])
    p.op('dve', lambda e: e.scalar_tensor_tensor(out=st.ysb[:], in0=xi[:], scalar=ALPHA, op0=ALU.mult,
                                                 in1=ps_tile, op1=ALU.add),
         reads=[('xin', blk % 2)] + ps_keys, writes=['ysb'])
    for i in range(2):
        p.op('dve', lambda e: e.bn_stats(out=st.lnst[:, i, :], in_=st.ysb[:, i * 512:(i + 1) * 512]),
             reads=['ysb'], writes=[('lnst', i)])
    p.op('dve', lambda e: e.bn_aggr(out=st.lnmv[:, 0:2], in_=st.lnst[:].rearrange("p a b -> p (a b)")),
         reads=[('lnst', 0), ('lnst', 1)], writes=['lnmv'])
    p.op('act', lambda e: e.activation(out=st.lnmv[:, 2:3], in_=st.lnmv[:, 1:2], func=AF.Ln, bias=st.eps[:], scale=1.0),
         reads=['lnmv', 'eps'], writes=['lnmv2'])
    p.op('act', lambda e: e.activation(out=st.lnmv[:, 3:4], in_=st.lnmv[:, 2:3], func=AF.Exp, scale=-0.5),
         reads=['lnmv2'], writes=['lnmv3'])
    p.op('dve', lambda e: e.tensor_scalar(out=st.ysb[:], in0=st.ysb[:], scalar1=st.lnmv[:, 0:1], scalar2=st.lnmv[:, 3:4],
                                          op0=ALU.subtract, op1=ALU.mult),
         reads=['ysb', 'lnmv', 'lnmv3'], writes=['ysb'])
    p.op('pool', lambda e: e.tensor_tensor(out=st.ysb[:], in0=st.ysb[:], in1=st.lnw[:, 0, :], op=ALU.mult),
         reads=['ysb', 'lng'], writes=['ysb'])
    p.op('pool', lambda e: e.tensor_tensor(out=xo[:], in0=st.ysb[:], in1=st.lnw[:, 1, :], op=ALU.add),
         reads=['ysb', 'lnb'], writes=[('xo', blk % 2)])
    p.dma('sp', x_dst[blk * 128:(blk + 1) * 128, :], xo[:], reads=[('xo', blk % 2)], writes=[(dkey, blk)])
    if do_xt:
        xt_from_tile(p, st, blk, xo, ('xo', blk % 2))


def load_w_bf16(p, dst, w_dram, key, ncols=1024, col0=0):
    src = w_dram.rearrange("(c p) n -> p c n", p=128)[:, :, col0:col0 + ncols]
    for c0 in range(0, 8, 2):
        p.dma('pool', dst[:, c0:c0 + 2, :], src[:, c0:c0 + 2, :], writes=[(key, c0)])
    return [(key, c0) for c0 in range(0, 8, 2)]


def rope_tables(p, st, pos_dram, n, scale):
    R = st.rope
    p.dma('sp', R['pos'][:, 0:n], pos_dram.to_broadcast([128, n]), writes=['rpos'])
    p.op('dve', lambda e: e.tensor_scalar(out=R['y'][:, 0:n], in0=R['pos'][:, 0:n], scalar1=st.ropec[:, 0:1],
                                          scalar2=None, op0=ALU.mult), reads=['rpos', 'ropec'], writes=['ry'])
    for which in range(2):
        y = R['y'][:, 0:n]
        r = R['r'][:, 0:n]
        if which == 0:
            p.op('dve', lambda e: e.tensor_scalar(out=R['y2'][:, 0:n], in0=y, scalar1=0.25, scalar2=None, op0=ALU.add),
                 reads=['ry'], writes=['ry2'])
            y = R['y2'][:, 0:n]
            ykey = 'ry2'
        else:
            ykey = 'ry'
        p.op('dve', lambda e: e.tensor_copy(out=R['yi'][:, 0:n], in_=y), reads=[ykey], writes=['ryi'])
        p.op('dve', lambda e: e.tensor_copy(out=R['yf'][:, 0:n], in_=R['yi'][:, 0:n]), reads=['ryi'], writes=['ryf'])
        p.op('dve', lambda e: e.tensor_tensor(out=r, in0=y, in1=R['yf'][:, 0:n], op=ALU.subtract),
             reads=[ykey, 'ryf'], writes=['rr'])
        p.op('dve', lambda e: e.tensor_scalar(out=R['yf'][:, 0:n], in0=r, scalar1=0.5, scalar2=None, op0=ALU.is_gt),
             reads=['rr'], writes=['ryf'])
        p.op('dve', lambda e: e.tensor_tensor(out=r, in0=r, in1=R['yf'][:, 0:n], op=ALU.subtract),
             reads=['rr', 'ryf'], writes=['rr'])
        p.op('dve', lambda e: e.tensor_scalar(out=R['yf'][:, 0:n], in0=r, scalar1=-0.5, scalar2=None, op0=ALU.is_lt),
             reads=['rr'], writes=['ryf'])
        p.op('dve', lambda e: e.tensor_tensor(out=r, in0=r, in1=R['yf'][:, 0:n], op=ALU.add),
             reads=['rr', 'ryf'], writes=['rr'])
        tab = R['cos'] if which == 0 else R['sin']
        tkey = 'rcos' if which == 0 else 'rsin'
        sc = st.ropec[:, 2:3] if which == 0 else st.ropec[:, 1:2]
        p.op('act', lambda e: e.activation(out=tab[:, 0:n], in_=r, func=AF.Sin, scale=sc),
             reads=['rr', 'ropec'], writes=[tkey])
        if scale != 1.0:
            p.op('dve', lambda e: e.tensor_scalar(out=tab[:, 0:n], in0=tab[:, 0:n], scalar1=scale, scalar2=None, op0=ALU.mult),
                 reads=[tkey], writes=[tkey])


def proj_rope(p, st, W, WR, wkeys, wrkeys, rhs_fn, rhs_keys, n, out_fn, out_keys_fn):
    R = st.rope
    for hc in range(8):
        b0, b1 = (hc % 2) * 2, (hc % 2) * 2 + 1
        for (Wt, wk, bk) in ((W, wkeys, b0), (WR, wrkeys, b1)):
            for dc in range(8):
                p.op('pe', lambda e: e.matmul(bank(st, bk)[:, 0:n], lhsT=Wt[:, dc, hc * 128:(hc + 1) * 128], rhs=rhs_fn(dc),
                                              start=(dc == 0), stop=(dc == 7)),
                     reads=wk + rhs_keys, writes=[('bank', bk)])
        t1 = R['t1'][hc % 2]
        t2 = R['t2'][hc % 2]
        p.op('dve', lambda e: e.tensor_tensor(out=t1[:, 0:n], in0=bank(st, b0)[:, 0:n], in1=R['cos'][:, 0:n], op=ALU.mult),
             reads=[('bank', b0), 'rcos'], writes=[('rt1', hc % 2)])
        p.op('dve', lambda e: e.tensor_tensor(out=t2[:, 0:n], in0=bank(st, b1)[:, 0:n], in1=R['sin'][:, 0:n], op=ALU.mult),
             reads=[('bank', b1), 'rsin'], writes=[('rt2', hc % 2)])
        p.op('pool', lambda e: e.tensor_tensor(out=out_fn(hc), in0=t1[:, 0:n], in1=t2[:, 0:n], op=ALU.add),
             reads=[('rt1', hc % 2), ('rt2', hc % 2)], writes=out_keys_fn(hc))


def group_tiles(g, descending):
    tiles = [(kb, None, 0) for kb in range(16 * g)]
    tiles += [(16 * g + u, u, u // 4) for u in range(16)]
    if descending:
        tiles = tiles[::-1]
    return tiles


def post_attn(p, st, WO, wokeys, x_src, skey, x_dst, dkey):
    OT = st.OT
    for blk in range(NB):
        pst = bank(st, 4).bitcast(BF16)
        for h in range(8):
            p.op('pe', lambda e: e.transpose(out=pst[:, h * 128:(h + 1) * 128], in_=st.OALL[:, blk, h * 128:(h + 1) * 128],
                                             identity=st.ident[:]),
                 reads=[('OALL', blk), 'ident'], writes=[('bank', 4)])
        p.op('act', lambda e: e.copy(out=OT[:].rearrange("p a b -> p (a b)"), in_=pst), reads=[('bank', 4)], writes=['OT'])
        for half in range(2):
            for h in range(8):
                p.op('pe', lambda e: e.matmul(bank(st, half), lhsT=OT[:, h, :], rhs=WO[:, h, half * 512:(half + 1) * 512],
                                              start=(h == 0), stop=(h == 7)),
                     reads=['OT'] + wokeys, writes=[('bank', half)])
        finish_block(p, st, blk, st.PS[0][:], [('bank', 0), ('bank', 1)], x_src, skey, x_dst, dkey)


def peer_convert(p, uT_dram, v_dram, uT_bf, v_bf, tag):
    keys = []
    us = uT_dram.rearrange("r (a b) -> (r a) b", b=2048)
    ud = uT_bf.rearrange("r (a b) -> (r a) b", b=2048)
    for i in range(8):
        p.dma('pool', ud[i * 1024:(i + 1) * 1024, :], us[i * 1024:(i + 1) * 1024, :], writes=[(tag + 'u', i)])
        keys.append((tag + 'u', i))
    for i in range(16):
        p.dma('pool', v_bf[i * 1024:(i + 1) * 1024, :], v_dram[i * 1024:(i + 1) * 1024, :], writes=[(tag + 'v', i)])
        keys.append((tag + 'v', i))
    return keys


def peer_layer(p, st, wpq_dram, subk_dram, uT_bf, v_bf, convkeys, g_dram, b_dram, x_src, skey, x_dst, dkey, do_xt):
    PE = st.peer
    load_ln(p, st, g_dram, b_dram)
    wpqk = load_w_bf16(p, PE['WPQ'], wpq_dram, 'WPQ')
    p.dma('pool', PE['SUBK'][:], subk_dram, writes=['SUBK'])
    p.op('pool', lambda e: e.memset(PE['QP'][:], 0.0), writes=['QP'])
    p.op('pool', lambda e: e.memset(PE['QPH'][:], 0.0), writes=['QPH'])
    uview = uT_bf.rearrange("(c p) e -> p c e", p=128)
    vview = v_bf.rearrange("(t i p) d -> t p i d", p=128, i=4)
    NT = NEXP // 512
    for blk in range(_DBG.get('peer_blocks', NB)):
        tok = slice(blk * 128, (blk + 1) * 128)
        xtk = xt_keys([blk])
        for h in range(8):
            bk = 4 + h // 4
            for dc in range(8):
                p.op('pe', lambda e: e.matmul(bank(st, bk)[:, (h % 4) * 128:(h % 4 + 1) * 128],
                                              lhsT=PE['WPQ'][:, dc, h * 128:(h + 1) * 128], rhs=st.XT[:, dc, tok],
                                              start=(dc == 0), stop=(dc == 7)),
                     reads=wpqk + xtk, writes=[('bank', bk)])
        p.op('act', lambda e: e.copy(out=PE['QP'][0:64, :, :].rearrange("p a b -> p (a b)"), in_=st.PS[2][0:64, :]),
             reads=[('bank', 4), ('bank', 5)], writes=['QP'])
        p.op('act', lambda e: e.copy(out=PE['QPH'][64:128, :, :].rearrange("p a b -> p (a b)"), in_=st.PS[2][64:128, :]),
             reads=[('bank', 4), ('bank', 5)], writes=['QPH'])
        for h in range(8):
            for c in range(2):
                j = h * 2 + c
                bk = 4 + j // 4
                p.op('pe', lambda e: e.matmul(bank(st, bk)[:, (j % 4) * 128:(j % 4 + 1) * 128],
                                              lhsT=(PE['QP'] if c == 0 else PE['QPH'])[:, h, :], rhs=PE['SUBK'][:, h, :],
                                              start=True, stop=True),
                     reads=['QP', 'QPH', 'SUBK'], writes=[('bank', bk)])
        for h in range(8):
            T = PE['T24']
            for c in range(2):
                j = h * 2 + c
                src = bank(st, 4 + j // 4)[:, (j % 4) * 128:(j % 4 + 1) * 128]
                sk = ('bank', 4 + j // 4)
                W1 = PE['W1']
                p.op('dve', lambda e: e.max(out=T[:, c, 0:8], in_=src), reads=[sk], writes=[('T24', c)])
                p.op('dve', lambda e: e.match_replace(out=W1[:], in_to_replace=T[:, c, 0:8], in_values=src, imm_value=-1e30),
                     reads=[sk, ('T24', c)], writes=['W1'])
                p.op('dve', lambda e: e.max(out=T[:, c, 8:16], in_=W1[:]), reads=['W1'], writes=[('T24', c)])
                p.op('dve', lambda e: e.match_replace(out=W1[:], in_to_replace=T[:, c, 8:16], in_values=W1[:], imm_value=-1e30),
                     reads=['W1', ('T24', c)], writes=['W1'])
                p.op('dve', lambda e: e.max(out=T[:, c, 16:24], in_=W1[:]), reads=['W1'], writes=[('T24', c)])
            W2 = PE['W2']
            p.op('dve', lambda e: e.tensor_tensor(out=W2[:], in0=T[:, 0, :].unsqueeze(2).to_broadcast([128, 24, 24]),
                                                  in1=T[:, 1, :].unsqueeze(1).to_broadcast([128, 24, 24]), op=ALU.add),
                 reads=[('T24', 0), ('T24', 1)], writes=['W2'])
            C = PE['C24']
            W2f = W2[:].rearrange("p a b -> p (a b)")
            p.op('dve', lambda e: e.max(out=C[:, 0:8], in_=W2f), reads=['W2'], writes=['C24'])
            p.op('dve', lambda e: e.match_replace(out=W2f, in_to_replace=C[:, 0:8], in_values=W2f, imm_value=-1e30),
                 reads=['W2', 'C24'], writes=['W2'])
            p.op('dve', lambda e: e.max(out=C[:, 8:16], in_=W2f), reads=['W2'], writes=['C24'])
            p.op('dve', lambda e: e.match_replace(out=W2f, in_to_replace=C[:, 8:16], in_values=W2f, imm_value=-1e30),
                 reads=['W2', 'C24'], writes=['W2'])
            p.op('dve', lambda e: e.max(out=C[:, 16:24], in_=W2f), reads=['W2'], writes=['C24'])
            sm = PE['SM']
            p.op('dve', lambda e: e.tensor_scalar(out=sm[:, 0:1], in0=C[:, 0:1], scalar1=-1.0, scalar2=None, op0=ALU.mult),
                 reads=['C24'], writes=['SM0'])
            p.op('act', lambda e: e.activation(out=PE['EX16'][:], in_=C[:, 0:16], func=AF.Exp, bias=sm[:, 0:1], scale=1.0,
                                               accum_out=sm[:, 1:2]),
                 reads=['C24', 'SM0'], writes=['EX16', 'SM1'])
            p.op('act', lambda e: e.activation(out=sm[:, 2:3], in_=sm[:, 1:2], func=AF.Ln), reads=['SM1'], writes=['SM2'])
            p.op('dve', lambda e: e.tensor_tensor(out=PE['BIAS'][:, h:h + 1], in0=sm[:, 0:1], in1=sm[:, 2:3], op=ALU.subtract),
                 reads=['SM0', 'SM2'], writes=[('BIAS', h)])
            p.op('dve', lambda e: e.tensor_tensor(out=sm[:, 3:4], in0=C[:, 15:16], in1=C[:, 16:17], op=ALU.add),
                 reads=['C24'], writes=['SM3'])
            p.op('dve', lambda e: e.tensor_scalar(out=PE['GTH'][:, h:h + 1], in0=sm[:, 3:4], scalar1=0.5, scalar2=None, op0=ALU.mult),
                 reads=['SM3'], writes=[('GTH', h)])
        if _DBG.get('peer_sub', 9) <= 1:
            continue
        def gates_head(k, h):
            GS = PE['GS'][k % 2]
            cb = 2 + (h % 4)
            p.op('pe', lambda e: e.matmul(bank(st, cb).rearrange("p (a b) -> p a b", a=4), lhsT=PE['QP'][:, h, :],
                                          rhs=PE['SUBK'][:, h, k * 4:k * 4 + 4].unsqueeze(2).to_broadcast([128, 4, 128]),
                                          start=True, stop=False),
                 reads=['QP', 'QPH', 'SUBK'], writes=[('bank', cb)])
            p.op('pe', lambda e: e.matmul(bank(st, cb).rearrange("p (a b) -> p a b", a=4), lhsT=PE['QPH'][:, h, :],
                                          rhs=PE['SUBK'][:, h, :].unsqueeze(1).to_broadcast([128, 4, 128]),
                                          start=False, stop=True),
                 reads=['QP', 'QPH', 'SUBK'], writes=[('bank', cb)])
            EF = PE['EF'][h % 4]
            p.op('act', lambda e: e.activation(out=EF[:], in_=bank(st, cb), func=AF.Exp, bias=PE['BIAS'][:, h:h + 1], scale=1.0),
                 reads=[('bank', cb), ('BIAS', h)], writes=[('EF', h % 4)])
            p.op('dve', lambda e: e.scalar_tensor_tensor(out=GS[:, h, :], in0=bank(st, cb), scalar=PE['GTH'][:, h:h + 1], op0=ALU.is_ge,
                                                         in1=EF[:], op1=ALU.mult),
                 reads=[('bank', cb), ('EF', h % 4), ('GTH', h)], writes=[('GS', k % 2, h)])
            if h in (1, 3, 5):
                p.op('pool', lambda e: e.tensor_tensor(out=GS[:, h - 1, :], in0=GS[:, h - 1, :], in1=GS[:, h, :], op=ALU.add),
                     reads=[('GS', k % 2, h - 1), ('GS', k % 2, h)], writes=[('GS', k % 2, h - 1)])

        def gates_tail(k):
            GS = PE['GS'][k % 2]
            gk = lambda h: ('GS', k % 2, h)
            p.op('dve', lambda e: e.tensor_tensor(out=GS[:, 6, :], in0=GS[:, 6, :], in1=GS[:, 7, :], op=ALU.add),
                 reads=[gk(6), gk(7)], writes=[gk(6)])
            p.op('dve', lambda e: e.tensor_tensor(out=GS[:, 0, :], in0=GS[:, 0, :], in1=GS[:, 2, :], op=ALU.add),
                 reads=[gk(0), gk(2)], writes=[gk(0)])
            p.op('dve', lambda e: e.tensor_tensor(out=GS[:, 4, :], in0=GS[:, 4, :], in1=GS[:, 6, :], op=ALU.add),
                 reads=[gk(4), gk(6)], writes=[gk(4)])
            p.op('dve', lambda e: e.tensor_tensor(out=PE['GA'][k % 3][:, 0:512], in0=GS[:, 0, :], in1=GS[:, 4, :], op=ALU.add),
                 reads=[gk(0), gk(4)], writes=[('GA', k % 3)])

        def expert_loads(k):
            ub = k % 3
            p.dma('sp', PE['UT'][ub][:], uview[:, :, k * 512:(k + 1) * 512], reads=convkeys, writes=[('UT', ub)])
            p.dma('sp', PE['VT'][k % 4][:], vview[k], reads=convkeys, writes=[('VT', k % 4)])

        def expert_chunk_a(k, c):
            ub = k % 4
            UT = PE['UT'][k % 3]
            GL = PE['GL'][k % 2]
            GH = PE['GH'][k % 2]
            if c < 4:
                for dc in (2 * c, 2 * c + 1):
                    p.op('pe', lambda e: e.matmul(bank(st, 6), lhsT=st.XT[:, dc, tok], rhs=UT[:, dc, :], start=(dc == 0), stop=(dc == 7)),
                         reads=xtk + [('UT', k % 3)], writes=[('bank', 6)])
            if c == 3:
                p.op('act', lambda e: e.activation(out=GL[:], in_=bank(st, 6), func=AF.Gelu), reads=[('bank', 6)], writes=[('GL', k % 2)])
            if c == 5:
                p.op('pool', lambda e: e.tensor_tensor(out=GH[:], in0=GL[:], in1=PE['GA'][k % 3][:, 0:512], op=ALU.mult),
                     reads=[('GL', k % 2), ('GA', k % 3)], writes=[('GH', k % 2)])
            if c == 7:
                ptr = bank(st, 7).bitcast(BF16)
                for i in range(4):
                    p.op('pe', lambda e: e.transpose(out=ptr[:, i * 128:(i + 1) * 128], in_=GH[:, i * 128:(i + 1) * 128], identity=st.ident[:]),
                         reads=[('GH', k % 2), 'ident'], writes=[('bank', 7)])

        def expert_chunk_b(k, c):
            VT = PE['VT'][k % 4]
            GHT = PE['GHT'][k % 2]
            if c == 1:
                ptr = bank(st, 7).bitcast(BF16)
                p.op('act', lambda e: e.copy(out=GHT[:].rearrange("p a b -> p (a b)"), in_=ptr[:, 0:512]),
                     reads=[('bank', 7)], writes=[('GHT', k % 2)])
            if c in (2, 3):
                half = c - 2
                for i in range(4):
                    p.op('pe', lambda e: e.matmul(bank(st, half), lhsT=GHT[:, i, :], rhs=VT[:, i, half * 512:(half + 1) * 512],
                                                  start=(k == 0 and i == 0), stop=(k == NT - 1 and i == 3)),
                         reads=[('GHT', k % 2), ('VT', k % 4)], writes=[('bank', half)])

        for k in range(NT + 2):
            if k < NT:
                expert_loads(k)
            for h in range(8):
                if k < NT:
                    gates_head(k, h)
                if 1 <= k <= NT:
                    expert_chunk_a(k - 1, h)
                if k >= 2:
                    expert_chunk_b(k - 2, h)
            if k < NT:
                gates_tail(k)
        if _DBG.get('peer_sub', 9) <= 3:
            continue
        finish_block(p, st, blk, st.PS[0][:], [('bank', 0), ('bank', 1)], x_src, skey, x_dst, dkey, do_xt)


def alloc_peer(p, st):
    PE = {}
    PE['WPQ'] = p.sb('WPQ', [128, 8, 1024], BF16)
    PE['SUBK'] = p.sb('SUBK', [128, 8, 128], BF16)
    PE['QP'] = p.sb('QP', [128, 8, 128], BF16)
    PE['QPH'] = p.sb('QPH', [128, 8, 128], BF16)
    PE['T24'] = p.sb('T24', [128, 2, 24], F32)
    PE['W1'] = p.sb('W1', [128, 128], F32)
    PE['W2'] = p.sb('W2', [128, 24, 24], F32)
    PE['C24'] = p.sb('C24', [128, 24], F32)
    PE['SM'] = p.sb('SM', [128, 8], F32)
    PE['EX16'] = p.sb('EX16', [128, 16], F32)
    PE['BIAS'] = p.sb('BIAS', [128, 8], F32)
    PE['GTH'] = p.sb('GTH', [128, 8], F32)
    PE['GA'] = [p.sb('GA%d' % i, [128, 512], BF16) for i in range(3)]
    PE['EF'] = [p.sb('EF%d' % i, [128, 512], BF16) for i in range(4)]
    PE['GS'] = [p.sb('GS%d' % i, [128, 8, 512], BF16) for i in range(2)]
    PE['UT'] = [p.sb('UT%d' % i, [128, 8, 512], BF16) for i in range(3)]
    PE['VT'] = [p.sb('VT%d' % i, [128, 4, 1024], BF16) for i in range(4)]
    PE['GL'] = [p.sb('GL%d' % i, [128, 512], BF16) for i in range(2)]
    PE['GH'] = [p.sb('GH%d' % i, [128, 512], BF16) for i in range(2)]
    PE['GHT'] = [p.sb('GHT%d' % i, [128, 4, 128], BF16) for i in range(2)]
    st.peer = PE


def alloc_rope(p, st):
    R = {}
    for nm in ['pos', 'y', 'y2', 'yf', 'r', 'cos', 'sin']:
        R[nm] = p.sb('rp_' + nm, [128, 512], F32)
    R['yi'] = p.sb('rp_yi', [128, 512], I32)
    R['t1'] = [p.sb('rp_t1%d' % i, [128, 512], F32) for i in range(2)]
    R['t2'] = [p.sb('rp_t2%d' % i, [128, 512], F32) for i in range(2)]
    st.rope = R


def diff_projections(p, st, d, KTs, Vs):
    with ExitStack() as es2:
        old = p.es
        p.es = es2
        alloc_rope(p, st)
        W0 = p.sb('W0', [128, 8, 1024], BF16)
        W1 = p.sb('W1p', [128, 8, 1024], BF16)
        W2 = p.sb('W2p', [128, 8, 1024], BF16)
        XK = [p.sb('XK%d' % i, [128, 8, 512], BF16) for i in range(2)]
        KTt = [p.sb('KTt%d' % i, [128, 8, 512], BF16) for i in range(2)]
        Vt = [p.sb('Vt0', [128, 4, 1024], BF16)] * 2
        k0 = load_w_bf16(p, W0, d['w_q'], 'W0')
        k1 = load_w_bf16(p, W1, d['w_qr'], 'W1')
        for tcn in range(4):
            sl = slice(tcn * 512, (tcn + 1) * 512)
            rope_tables(p, st, d['pos_q'][0:1, sl], 512, 0.125)
            proj_rope(p, st, W0, W1, k0, k1, lambda dc: st.XT[:, dc, sl], xt_keys(range(tcn * 4, tcn * 4 + 4)), 512,
                      lambda hc: st.QT[:, hc, sl], lambda hc: [('QT', hc, tcn)])
        k0 = load_w_bf16(p, W0, d['w_k'], 'W0')
        k1 = load_w_bf16(p, W1, d['w_kr'], 'W1')
        k2 = load_w_bf16(p, W2, d['w_v'], 'W2')
        xTv = d['xT'].rearrange("(c p) t -> p c t", p=128)
        KTv = KTs.rearrange("(c p) t -> p c t", p=128)
        Vv = Vs.rearrange("(b p) n -> p b n", p=128)
        for tcn in range(16):
            sl = slice(tcn * 512, (tcn + 1) * 512)
            xk = XK[tcn % 2]
            xkk = ('XK', tcn % 2)
            for c0 in range(0, 8, 4):
                p.dma('pool', xk[:, c0:c0 + 4, :], xTv[:, c0:c0 + 4, sl], writes=[(xkk, c0)])
            xkeys = [(xkk, 0), (xkk, 4)]
            rope_tables(p, st, d['pos_k'][0:1, sl], 512, 1.0)
            kt = KTt[tcn % 2]
            proj_rope(p, st, W0, W1, k0, k1, lambda dc: xk[:, dc, :], xkeys, 512,
                      lambda hc: kt[:, hc, :], lambda hc: [('KTt', tcn % 2, hc)])
            p.dma('sp', KTv[:, :, sl], kt[:], reads=[('KTt', tcn % 2, hc) for hc in range(8)], writes=[('KTs', tcn)])
            vt = Vt[tcn % 2]
            for tb in range(4):
                for half in range(2):
                    bk = 4 + (tb * 2 + half) % 4
                    for dc in range(8):
                        p.op('pe', lambda e: e.matmul(bank(st, bk), lhsT=xk[:, dc, tb * 128:(tb + 1) * 128],
                                                      rhs=W2[:, dc, half * 512:(half + 1) * 512], start=(dc == 0), stop=(dc == 7)),
                             reads=xkeys + k2, writes=[('bank', bk)])
                    p.op('act', lambda e: e.copy(out=vt[:, tb, half * 512:(half + 1) * 512], in_=bank(st, bk)),
                         reads=[('bank', bk)], writes=[('Vt', 0, tb, half)])
            p.dma('sp', Vv[:, tcn * 4:(tcn + 1) * 4, :], vt[:],
                  reads=[('Vt', 0, tb, half) for tb in range(4) for half in range(2)], writes=[('Vs', tcn)])
        p.barrier()
        p.es = old


def attn_diff(p, st, d, KTs, Vs):
    with ExitStack() as es2:
        old = p.es
        p.es = es2
        KT = p.sb('KTh', [128, S], BF16)
        VH = p.sb('VHh', [128, 64, 132], BF16)
        MK = p.sb('MK', [128, 16, 512], BF16)
        E = [p.sb('E%d' % i, [128, 512], BF16) for i in range(4)]
        lam = p.sb('lam', [128, 4, 64], F32)
        lsm = p.sb('lsm', [128, 8], F32)
        gsc = p.sb('gsc', [128, 128], F32)
        ep = p.sb('ep', [128, 8], F32)
        d0 = p.sb('d0', [128, 128], F32)
        dd = p.sb('dd', [128, 128], F32)
        junk = p.sb('junk', [128, 128], F32)
        QZ = [p.sb('QZd%d' % i, [128, TQ], BF16) for i in range(2)]
        for i in range(2):
            p.op('pool', lambda e: e.memset(QZ[i][:], 0.0), writes=[('QZ', i)])
        p.dma('sp', MK[:], d['mask'].rearrange("u k q -> k u q"), writes=['MK'])
        p.op('pool', lambda e: e.memset(VH[:, :, 128:132], 1.0), writes=['VHones'])
        p.dma('sp', lam[:].rearrange("p a b -> p (a b)"), d['lam'][0:1, :].to_broadcast([128, 256]), writes=['lam'])
        p.dma('sp', gsc[:], d['subg'][0:1, :].to_broadcast([128, 128]), writes=['gsc'])
        lam_init = 0.8 - 0.6 * math.exp(0.0)
        for i in range(2):
            p.op('dve', lambda e: e.tensor_tensor(out=junk[:, 0:64], in0=lam[:, 2 * i, :], in1=lam[:, 2 * i + 1, :], op=ALU.mult),
                 reads=['lam'], writes=['junk'])
            p.op('dve', lambda e: e.tensor_reduce(out=lsm[:, i:i + 1], in_=junk[:, 0:64], op=ALU.add, axis=mybir.AxisListType.X),
                 reads=['junk'], writes=[('lsm', i)])
            p.op('act', lambda e: e.activation(out=lsm[:, 2 + i:3 + i], in_=lsm[:, i:i + 1], func=AF.Exp),
                 reads=[('lsm', i)], writes=[('lsm', 2 + i)])
        p.op('dve', lambda e: e.tensor_tensor(out=lsm[:, 4:5], in0=lsm[:, 3:4], in1=lsm[:, 2:3], op=ALU.subtract),
             reads=[('lsm', 2), ('lsm', 3)], writes=[('lsm', 4)])
        p.op('dve', lambda e: e.tensor_scalar(out=lsm[:, 5:6], in0=lsm[:, 4:5], scalar1=-lam_init, scalar2=None, op0=ALU.add),
             reads=[('lsm', 4)], writes=['neglam'])
        p.op('dve', lambda e: e.tensor_scalar(out=gsc[:], in0=gsc[:], scalar1=1.0 - lam_init, scalar2=None, op0=ALU.mult),
             reads=['gsc'], writes=['gsc'])
        Vv = Vs.rearrange("(b p) n -> p b n", p=128)
        ei = 0
        for h in range(8):
            for q4 in range(4):
                p.dma('sp', KT[:, q4 * 2048:(q4 + 1) * 2048], KTs[h * 128:(h + 1) * 128, q4 * 2048:(q4 + 1) * 2048],
                      reads=[('KTs', t) for t in range(q4 * 4, q4 * 4 + 4)], writes=[('KT', q4)])
                p.dma('sp', VH[:, q4 * 16:(q4 + 1) * 16, 0:128], Vv[:, q4 * 16:(q4 + 1) * 16, h * 128:(h + 1) * 128],
                      reads=[('Vs', t) for t in range(q4 * 4, q4 * 4 + 4)], writes=[('VH', q4)])
            for mp_ in range(2):
                p.op('pool', lambda e: e.tensor_copy(out=QZ[mp_][mp_ * 64:(mp_ + 1) * 64, :], in_=st.QT[mp_ * 64:(mp_ + 1) * 64, h, :]),
                     reads=[('QT', h, t) for t in range(4)], writes=[('QZ', mp_)])
            for g in range(4):
                tiles = []
                for (kb, u, amin) in group_tiles(g, False):
                    for mp in range(2):
                        tiles.append((kb, u, amin, mp))
                nt = len(tiles)
                lastkb = [16 * g + 4 * a + 3 for a in range(4)]
                slots = {}

                def stage1(i):
                    kb, u, amin, mp = tiles[i]
                    n = 512 - 128 * amin
                    sb_ = 4 + (i % 4)
                    slots[i] = sb_
                    p.op('pe', lambda e: e.matmul(bank(st, sb_)[:, 0:n], lhsT=KT[:, kb * 128:(kb + 1) * 128],
                                                  rhs=QZ[mp][:, g * 512 + amin * 128:(g + 1) * 512],
                                                  start=True, stop=True),
                         reads=[('KT', kb // 16), ('QZ', mp)], writes=[('bank', sb_)])

                def stage2(i):
                    kb, u, amin, mp = tiles[i]
                    n = 512 - 128 * amin
                    sb_ = slots[i]
                    Et = E[i % 4]
                    p.op('act', lambda e: e.activation(out=Et[:, 0:n], in_=bank(st, sb_)[:, 0:n], func=AF.Exp),
                         reads=[('bank', sb_)], writes=[('E', i % 4)])
                    if u is not None:
                        p.op('pool', lambda e: e.tensor_tensor(out=Et[:, 0:n], in0=Et[:, 0:n], in1=MK[:, u, amin * 128:512], op=ALU.mult),
                             reads=[('E', i % 4), 'MK'], writes=[('E', i % 4)])

                def stage3(i):
                    kb, u, amin, mp = tiles[i]
                    Et = E[i % 4]
                    for a in range(amin, 4):
                        p.op('pe', lambda e: e.matmul(bank(st, a)[:, mp * 256:mp * 256 + 129], lhsT=Et[:, (a - amin) * 128:(a - amin + 1) * 128],
                                                      rhs=VH[:, kb, 0:129], start=(kb == 0 and mp == 0), stop=(kb == lastkb[a]),
                                                      skip_group_check=True),
                             reads=[('E', i % 4), ('VH', kb // 16), 'VHones'], writes=[('acc', a)])
                for i in range(nt + 2):
                    if i < nt:
                        stage1(i)
                    if 1 <= i <= nt:
                        stage2(i - 1)
                    if i >= 2:
                        stage3(i - 2)
                for a in range(4):
                    blk = g * 4 + a
                    acc = bank(st, a)
                    ak = ('acc', a)
                    p.op('dve', lambda e: e.reciprocal(out=ep[:, 0:1], in_=acc[:, 128:129]), reads=[ak], writes=[('ep', 0)])
                    p.op('dve', lambda e: e.reciprocal(out=ep[:, 1:2], in_=acc[:, 384:385]), reads=[ak], writes=[('ep', 1)])
                    p.op('dve', lambda e: e.tensor_tensor(out=ep[:, 2:3], in0=ep[:, 1:2], in1=lsm[:, 5:6], op=ALU.mult),
                         reads=[('ep', 1), 'neglam'], writes=[('ep', 2)])
                    p.op('dve', lambda e: e.tensor_scalar(out=d0[:], in0=acc[:, 0:128], scalar1=ep[:, 0:1], scalar2=None, op0=ALU.mult),
                         reads=[ak, ('ep', 0)], writes=['d0'])
                    p.op('dve', lambda e: e.scalar_tensor_tensor(out=dd[:], in0=acc[:, 256:384], scalar=ep[:, 2:3], op0=ALU.mult,
                                                                 in1=d0[:], op1=ALU.add),
                         reads=[ak, ('ep', 2), 'd0'], writes=['dd'])
                    p.op('act', lambda e: e.activation(out=junk[:], in_=dd[:], func=AF.Square, accum_out=ep[:, 3:4]),
                         reads=['dd'], writes=['junk', ('ep', 3)])
                    p.op('act', lambda e: e.activation(out=ep[:, 4:5], in_=ep[:, 3:4], func=AF.Ln, bias=st.eps[:], scale=1.0 / 128.0),
                         reads=[('ep', 3), 'eps'], writes=[('ep', 4)])
                    p.op('act', lambda e: e.activation(out=ep[:, 5:6], in_=ep[:, 4:5], func=AF.Exp, scale=-0.5),
                         reads=[('ep', 4)], writes=[('ep', 5)])
                    p.op('dve', lambda e: e.scalar_tensor_tensor(out=st.OALL[:, blk, h * 128:(h + 1) * 128], in0=dd[:], scalar=ep[:, 5:6],
                                                                 op0=ALU.mult, in1=gsc[:], op1=ALU.mult),
                         reads=['dd', ('ep', 5), 'gsc'], writes=[('OALL', blk)])
        p.barrier()
        p.es = old


def sb_q_projection(p, st, wq_dram):
    with ExitStack() as es2:
        old = p.es
        p.es = es2
        W0 = p.sb('W0q', [128, 8, 1024], BF16)
        k0 = load_w_bf16(p, W0, wq_dram, 'W0q')
        for tcn in range(4):
            sl = slice(tcn * 512, (tcn + 1) * 512)
            for hc in range(8):
                bk = 4 + hc % 2
                for dc in range(8):
                    p.op('pe', lambda e: e.matmul(bank(st, bk), lhsT=W0[:, dc, hc * 128:(hc + 1) * 128], rhs=st.XT[:, dc, sl],
                                                  start=(dc == 0), stop=(dc == 7)),
                         reads=k0 + xt_keys(range(tcn * 4, tcn * 4 + 4)), writes=[('bank', bk)])
                p.op('act', lambda e: e.activation(out=st.QT[:, hc, sl], in_=bank(st, bk), func=AF.Copy, scale=0.125),
                     reads=[('bank', bk)], writes=[('QT', hc, tcn)])
        p.barrier()
        p.es = old


def kv_projection(p, st, wkv_dram, kT_out, v_out):
    with ExitStack() as es2:
        old = p.es
        p.es = es2
        W0 = p.sb('Wk', [128, 8, 1024], BF16)
        W1 = p.sb('Wv', [128, 8, 1024], BF16)
        ko = [p.sb('ko%d' % i, [128, 512], BF16) for i in range(2)]
        k0 = load_w_bf16(p, W0, wkv_dram, 'Wk', col0=0)
        k1 = load_w_bf16(p, W1, wkv_dram, 'Wv', col0=1024)
        cnt = 0
        for tcn in range(4):
            sl = slice(tcn * 512, (tcn + 1) * 512)
            xk = xt_keys(range(tcn * 4, tcn * 4 + 4))
            for hc in range(8):
                bk = 4 + cnt % 2
                t = ko[cnt % 2]
                for dc in range(8):
                    p.op('pe', lambda e: e.matmul(bank(st, bk), lhsT=W0[:, dc, hc * 128:(hc + 1) * 128], rhs=st.XT[:, dc, sl],
                                                  start=(dc == 0), stop=(dc == 7)), reads=k0 + xk, writes=[('bank', bk)])
                p.op('act', lambda e: e.copy(out=t[:], in_=bank(st, bk)), reads=[('bank', bk)], writes=[('ko', cnt % 2)])
                p.dma('sp', kT_out[hc // 2][(hc % 2) * 128:(hc % 2 + 1) * 128, sl], t[:], reads=[('ko', cnt % 2)], writes=[('kTo', tcn, hc)])
                cnt += 1
            for tb in range(4):
                blk = tcn * 4 + tb
                for half in range(2):
                    bk = 4 + cnt % 2
                    t = ko[cnt % 2]
                    for dc in range(8):
                        p.op('pe', lambda e: e.matmul(bank(st, bk), lhsT=st.XT[:, dc, blk * 128:(blk + 1) * 128],
                                                      rhs=W1[:, dc, half * 512:(half + 1) * 512], start=(dc == 0), stop=(dc == 7)),
                             reads=k1 + xt_keys([blk]), writes=[('bank', bk)])
                    p.op('act', lambda e: e.copy(out=t[:], in_=bank(st, bk)), reads=[('bank', bk)], writes=[('ko', cnt % 2)])
                    p.dma('sp', v_out[blk // 4][(blk % 4) * 128:(blk % 4 + 1) * 128, half * 512:(half + 1) * 512], t[:], reads=[('ko', cnt % 2)],
                          writes=[('vo', blk, half)])
                    cnt += 1
        p.barrier()
        p.es = old
    return [('kTo', t, h) for t in range(4) for h in range(8)] + [('vo', b, h) for b in range(NB) for h in range(2)]


def sidx(kb):
    return (kb % 4) * 16 + kb // 4


def attn_sb(p, st, d, KTg, Vg, kvkeys):
    with ExitStack() as es2:
        old = p.es
        p.es = es2
        KT = p.sb('KTp', [128, S], BF16)
        VP = p.sb('VPp', [128, 64, 128], BF16)
        MK = p.sb('MKs', [128, 16, 512], BF16)
        NR = 4
        EZ = [p.sb('EZ0', [128, 512], F32)] * NR
        LP = [p.sb('LP%d' % i, [128, 512], F32) for i in range(NR)]
        WT = [p.sb('WT%d' % i, [128, 512], F32) for i in range(NR)]
        LM = [p.sb('LM%d' % i, [128, 512], BF16) for i in range(NR)]
        AT = [p.sb('AT%d' % i, [128, 512], BF16) for i in range(NR)]
        TRI = p.sb('TRI', [128, 128], BF16)
        ONES = p.sb('ONES', [128, 128], BF16)
        trif = p.sb('trif', [128, 128], F32)
        QZ1 = p.sb('QZs', [128, TQ], BF16)
        QZ = [QZ1, QZ1]
        p.dma('sp', MK[:], d['mask'].rearrange("u k q -> k u q"), writes=['MKs'])
        p.op('pool', lambda e: e.memset(trif[:], 1.0), writes=['trif'])
        p.op('pool', lambda e: e.affine_select(out=trif[:], in_=trif[:], pattern=[[-1, 128]], compare_op=ALU.is_ge, fill=0.0,
                                               base=-1, channel_multiplier=1), reads=['trif'], writes=['trif'])
        p.op('dve', lambda e: e.tensor_scalar(out=TRI[:], in0=trif[:], scalar1=-1.0, scalar2=None, op0=ALU.mult), reads=['trif'], writes=['TRI'])
        p.op('pool', lambda e: e.memset(ONES[:], 1.0), writes=['ONES'])
        for hp in range(8):
            for q4 in range(4):
                r0 = q4 * 256 + (hp % 2) * 128
                p.dma('sp', KT[:, q4 * 2048:(q4 + 1) * 2048], KTg[hp // 2][r0:r0 + 128, :],
                      reads=kvkeys, writes=[('KTp', q4)])
                for i in range(4):
                    vsrc = Vg[i][q4 * 512:(q4 + 1) * 512, hp * 128:(hp + 1) * 128].rearrange("(b p) n -> p b n", p=128)
                    p.dma('sp', VP[:, q4 * 16 + 4 * i:q4 * 16 + 4 * i + 4, :], vsrc, reads=kvkeys, writes=[('VPp', q4, i)])
            for hh in range(2):
                head = hp * 2 + hh
                rows = slice(hh * 64, (hh + 1) * 64)
                orow = slice((1 - hh) * 64, (2 - hh) * 64)
                p.op('pool', lambda e: e.memset(QZ1[orow, :], 0.0), writes=[('QZ', 0)])
                p.op('pool', lambda e: e.tensor_copy(out=QZ1[rows, :], in_=st.QT[rows, hp, :]),
                     reads=[('QT', hp, t) for t in range(4)] + [('QZ', 0)], writes=[('QZ', 0)])
                for g in range(4):
                    tiles = group_tiles(g, True)
                    nt = len(tiles)
                    CB = bank(st, 7)
                    p.op('dve', lambda e: e.memset(CB, 0.0), writes=[('bank', 7)])
                    zs = {}

                    def s1(i):
                        kb, u, amin = tiles[i]
                        n = 512 - 128 * amin
                        zb = 1 + (i % 6)
                        zs[i] = zb
                        p.op('pe', lambda e: e.matmul(bank(st, zb)[:, 0:n], lhsT=KT[:, sidx(kb) * 128:(sidx(kb) + 1) * 128],
                                                      rhs=QZ[hh][:, g * 512 + amin * 128:(g + 1) * 512], start=True, stop=False),
                             reads=[('KTp', sidx(kb) // 16), ('QZ', 0)], writes=[('bank', zb)])

                    def s2(i):
                        kb, u, amin = tiles[i]
                        n = 512 - 128 * amin
                        zb = zs[i]
                        r = i % NR
                        p.op('act', lambda e: e.activation(out=EZ[r][:, 0:n], in_=bank(st, zb)[:, 0:n], func=AF.Exp),
                             reads=[('bank', zb)], writes=[('EZ', 0)])
                        p.op('act', lambda e: e.activation(out=LP[r][:, 0:n], in_=EZ[r][:, 0:n], func=AF.Ln, bias=st.one[:], scale=1.0),
                             reads=[('EZ', 0), 'one'], writes=[('LP', r)])
                        if u is not None:
                            p.op('dve', lambda e: e.tensor_tensor(out=LM[r][:, 0:n], in0=LP[r][:, 0:n], in1=MK[:, u, amin * 128:512], op=ALU.mult),
                                 reads=[('LP', r), 'MKs'], writes=[('LM', r)])
                        else:
                            p.op('dve', lambda e: e.tensor_copy(out=LM[r][:, 0:n], in_=LP[r][:, 0:n]), reads=[('LP', r)], writes=[('LM', r)])

                    def s3(i):
                        kb, u, amin = tiles[i]
                        n = 512 - 128 * amin
                        c0 = amin * 128
                        r = i % NR
                        zb = zs[i]
                        p.op('pe', lambda e: e.matmul(bank(st, zb)[:, 0:n], lhsT=TRI[:], rhs=LM[r][:, 0:n], start=False, stop=True),
                             reads=['TRI', ('LM', r)], writes=[('bank', zb)])
                        p.op('dve', lambda e: e.tensor_tensor(out=WT[r][:, 0:n], in0=bank(st, zb)[:, 0:n], in1=LP[r][:, 0:n], op=ALU.subtract),
                             reads=[('bank', zb), ('LP', r)], writes=[('WT', r)])
                        if i > 0:
                            p.op('dve', lambda e: e.tensor_tensor(out=WT[r][:, 0:n], in0=WT[r][:, 0:n], in1=CB[:, c0:512], op=ALU.subtract),
                                 reads=[('WT', r), ('bank', 7)], writes=[('WT', r)])

                    def s3c(i):
                        kb, u, amin = tiles[i]
                        n = 512 - 128 * amin
                        c0 = amin * 128
                        r = i % NR
                        p.op('pe', lambda e: e.matmul(CB[:, c0:512], lhsT=ONES[:], rhs=LM[r][:, 0:n], start=(i == 0), stop=(i == nt - 1),
                                                      skip_group_check=True),
                             reads=['ONES', ('LM', r)], writes=[('bank', 7)])

                    def s4(i):
                        kb, u, amin = tiles[i]
                        n = 512 - 128 * amin
                        r = i % NR
                        p.op('act', lambda e: e.activation(out=AT[r][:, 0:n], in_=WT[r][:, 0:n], func=AF.Exp),
                             reads=[('WT', r)], writes=[('AT', r)])
                        if u is not None:
                            p.op('pool', lambda e: e.tensor_tensor(out=AT[r][:, 0:n], in0=AT[r][:, 0:n], in1=MK[:, u, amin * 128:512], op=ALU.mult),
                                 reads=[('AT', r), 'MKs'], writes=[('AT', r)])

                    def s5(i):
                        kb, u, amin = tiles[i]
                        r = i % NR
                        for a in range(amin, 4):
                            p.op('pe', lambda e: e.matmul(bank(st, 0)[:, a * 64:(a + 1) * 64], lhsT=AT[r][:, (a - amin) * 128:(a - amin + 1) * 128],
                                                          rhs=VP[:, sidx(kb), hh * 64:(hh + 1) * 64], start=(i == 0 and a == 3), stop=(kb == 0),
                                                          skip_group_check=True),
                                 reads=[('AT', r), ('VPp', sidx(kb) // 16, (sidx(kb) % 16) // 4)], writes=[('bank', 0)])
                    for j in range(nt + 5):
                        if j < nt:
                            s1(j)
                        if 0 <= j - 1 < nt:
                            s2(j - 1)
                        if 0 <= j - 3 < nt:
                            s3c(j - 3)
                        if 0 <= j - 2 < nt:
                            s3(j - 2)
                        if 0 <= j - 3 < nt:
                            s4(j - 3)
                        if 0 <= j - 4 < nt:
                            s5(j - 4)
                    for a in range(4):
                        blk = g * 4 + a
                        p.op('act', lambda e: e.copy(out=st.OALL[:, blk, head * 64:(head + 1) * 64], in_=bank(st, 0)[:, a * 64:(a + 1) * 64]),
                             reads=[('bank', 0)], writes=[('OALL', blk)])
        p.barrier()
        p.es = old


def _din(nc, name, shape, dt=F32):
    return nc.dram_tensor(name, list(shape), dt, kind="ExternalInput").ap()


def _dout(nc, name, shape, dt=F32):
    return nc.dram_tensor(name, list(shape), dt, kind="ExternalOutput").ap()


def _dscr(nc, name, shape, dt):
    return nc.dram_tensor(name, list(shape), dt, kind="Internal").ap()


PEER_IN = [('wpq', (1024, 1024)), ('subk', (128, 8, 128)), ('uT', (1024, NEXP)), ('v', (NEXP, 1024)),
           ('lng0', (1, D)), ('lnb0', (1, D)), ('lng1', (1, D)), ('lnb1', (1, D)), ('w_o', (1024, 1024))]


def build_A(stage=99):
    nc = bass.Bass("TRN2", target_bir_lowering=False)
    dbg = stage < 99
    _scr = _dout if dbg else _dscr
    d = {}
    for nm, shp in [('x_own', (TQ, D)), ('xT', (D, S)), ('pos_q', (1, TQ)), ('pos_k', (1, S)), ('ropec', (128, 3)),
                    ('w_q', (D, D)), ('w_qr', (D, D)), ('w_k', (D, D)), ('w_kr', (D, D)), ('w_v', (D, D)),
                    ('lam', (1, 256)), ('subg', (1, 128)), ('w_kv', (D, 2 * D))] + PEER_IN:
        if stage <= 3 and nm in ('uT', 'v'):
            d[nm] = _dscr(nc, nm, shp, F32)
            continue
        d[nm] = _din(nc, nm, shp)
    d['mask'] = _din(nc, 'mask', (16, 128, 512), BF16)
    x1 = _dout(nc, 'x1', (TQ, D))
    kT = _dout(nc, 'kT', (D, TQ))
    vv = _dout(nc, 'vv', (TQ, D))
    KTs = _scr(nc, 'KTs', (D, S), BF16)
    Vs = _scr(nc, 'Vs', (S, D), BF16)
    Xmid = _scr(nc, 'Xmid', (TQ, D), F32)
    uT_bf = _dscr(nc, 'uT_bf', (D, NEXP), BF16)
    v_bf = _dscr(nc, 'v_bf', (NEXP, D), BF16)
    with ExitStack() as es:
        p = Prog(nc, es)
        st = St()
        setup_common(p, st)
        st.ropec = p.sb('ropec', [128, 3], F32)
        p.dma('sp', st.ropec[:], d['ropec'], writes=['ropec'])
        make_xt(p, st, d['x_own'], 'x_own')
        with ExitStack() as esa:
            p.es = esa
            st.QT = p.sb('QT', [128, 8, TQ], BF16)
            diff_projections(p, st, d, KTs, Vs)
            if stage == 1:
                dq = _dout(nc, 'dbg_QT', (128, 8, TQ), BF16)
                dx = _dout(nc, 'dbg_XT', (128, 8, TQ), BF16)
                p.dma('sp', dq, st.QT[:], reads=[('QT', h, t) for h in range(8) for t in range(4)], writes=['dq'])
                p.dma('sp', dx, st.XT[:], reads=xt_keys(range(NB)), writes=['dx'])
                p.wait_all('sp', ['dq', 'dx'] + [('KTs', t) for t in range(16)] + [('Vs', t) for t in range(16)])
                return nc
            st.OALL = p.sb('OALL', [128, NB, D], BF16)
            st.OT = p.sb('OT', [128, 8, 128], BF16)
            convkeys = peer_convert(p, d['uT'], d['v'], uT_bf, v_bf, 'c0')
            attn_diff(p, st, d, KTs, Vs)
            if stage == 2:
                do = _dout(nc, 'dbg_OALL', (128, NB, D), BF16)
                p.dma('sp', do, st.OALL[:], reads=[('OALL', b) for b in range(NB)], writes=['do'])
                p.wait_all('sp', ['do'])
                return nc
            WO = p.sb('WO', [128, 8, 1024], BF16)
            wok = load_w_bf16(p, WO, d['w_o'], 'WO')
            load_ln(p, st, d['lng0'], d['lnb0'])
            post_attn(p, st, WO, wok, d['x_own'], 'x_own', Xmid, 'Xmid')
            p.barrier()
            if stage == 3:
                dx = _dout(nc, 'dbg_XT', (128, 8, TQ), BF16)
                p.dma('sp', dx, st.XT[:], reads=xt_keys(range(NB)), writes=['dx'])
                p.wait_all('sp', ['dx'] + [('Xmid', b) for b in range(NB)])
                return nc
            p.es = es
        with ExitStack() as esb:
            p.es = esb
            alloc_peer(p, st)
            peer_layer(p, st, d['wpq'], d['subk'], uT_bf, v_bf, convkeys, d['lng1'], d['lnb1'], Xmid, 'Xmid', x1, 'x1', True)
            p.barrier()
            p.es = es
        if stage == 4:
            p.wait_all('sp', [('x1', b) for b in range(_DBG.get('peer_blocks', NB))])
            return nc
        okeys = kv_projection(p, st, d['w_kv'], kT, vv)
        p.wait_all('sp', okeys + [('x1', b) for b in range(NB)])
        print("build_A ops", p.nops, "waits", p.nwaits, flush=True)
    return nc


def build_F():
    nc = bass.Bass("TRN2", target_bir_lowering=False)
    d = {}
    for nm, shp in [('x_own', (TQ, D)), ('xT', (D, S)), ('pos_q', (1, TQ)), ('pos_k', (1, S)), ('ropec', (128, 3)),
                    ('w_q', (D, D)), ('w_qr', (D, D)), ('w_k', (D, D)), ('w_kr', (D, D)), ('w_v', (D, D)),
                    ('lam', (1, 256)), ('subg', (1, 128)), ('w_kv', (D, 2 * D)), ('w_qb', (D, D))]:
        d[nm] = _din(nc, nm, shp)
    d0, d1 = {}, {}
    for nm, shp in PEER_IN:
        d0[nm] = _din(nc, nm + '_0', shp)
        d1[nm] = _din(nc, nm + '_1', shp)
    d['mask'] = _din(nc, 'mask', (16, 128, 512), BF16)
    dsb = {'mask': _din(nc, 'mask_sb', (16, 128, 512), BF16)}
    out = _dout(nc, 'out', (TQ, D))
    KTs = _dscr(nc, 'KTs', (D, S), BF16)
    Vs = _dscr(nc, 'Vs', (S, D), BF16)
    Xmid = _dscr(nc, 'Xmid', (TQ, D), F32)
    X1 = _dscr(nc, 'X1', (TQ, D), F32)
    uT_bf = [_dscr(nc, 'uT_bf%d' % l, (D, NEXP), BF16) for l in range(2)]
    v_bf = [_dscr(nc, 'v_bf%d' % l, (NEXP, D), BF16) for l in range(2)]
    kT_loc = [_dscr(nc, 'kT_loc%d' % i, (256, TQ), BF16) for i in range(4)]
    vv_loc = [_dscr(nc, 'vv_loc%d' % i, (512, D), BF16) for i in range(4)]
    KTg = [_dscr(nc, 'KTg%d' % i, (4 * 256, TQ), BF16) for i in range(4)]
    Vg = [_dscr(nc, 'Vg%d' % i, (4 * 512, D), BF16) for i in range(4)]
    groups = [[0, 1, 2, 3], [4, 5, 6, 7]]
    with ExitStack() as es:
        p = Prog(nc, es)
        st = St()
        setup_common(p, st)
        st.ropec = p.sb('ropec', [128, 3], F32)
        p.dma('sp', st.ropec[:], d['ropec'], writes=['ropec'])
        make_xt(p, st, d['x_own'], 'x_own')
        with ExitStack() as esa:
            p.es = esa
            st.QT = p.sb('QT', [128, 8, TQ], BF16)
            diff_projections(p, st, d, KTs, Vs)
            st.OALL = p.sb('OALL', [128, NB, D], BF16)
            st.OT = p.sb('OT', [128, 8, 128], BF16)
            conv0 = peer_convert(p, d0['uT'], d0['v'], uT_bf[0], v_bf[0], 'c0')
            attn_diff(p, st, d, KTs, Vs)
            conv1 = peer_convert(p, d1['uT'], d1['v'], uT_bf[1], v_bf[1], 'c1')
            WO = p.sb('WO', [128, 8, 1024], BF16)
            wok = load_w_bf16(p, WO, d0['w_o'], 'WO')
            load_ln(p, st, d0['lng0'], d0['lnb0'])
            post_attn(p, st, WO, wok, d['x_own'], 'x_own', Xmid, 'Xmid')
            p.barrier()
            p.es = es
        with ExitStack() as esb:
            p.es = esb
            alloc_peer(p, st)
            peer_layer(p, st, d0['wpq'], d0['subk'], uT_bf[0], v_bf[0], conv0, d0['lng1'], d0['lnb1'], Xmid, 'Xmid', X1, 'X1', True)
            p.barrier()
            p.es = es
        okeys = kv_projection(p, st, d['w_kv'], kT_loc, vv_loc)
        p.barrier()
        gkeys = []
        for i in range(4):
            p.collective("AllGather", groups, kT_loc[i], KTg[i], reads=okeys, writes=[('KTg', i)])
            p.collective("AllGather", groups, vv_loc[i], Vg[i], reads=okeys, writes=[('Vg', i)])
            gkeys += [('KTg', i), ('Vg', i)]
        p.barrier()
        with ExitStack() as esa:
            p.es = esa
            st.QT = p.sb('QT1', [128, 8, TQ], BF16)
            sb_q_projection(p, st, d['w_qb'])
            st.OALL = p.sb('OALL1', [128, NB, D], BF16)
            st.OT = p.sb('OT1', [128, 8, 128], BF16)
            attn_sb(p, st, dsb, KTg, Vg, gkeys)
            WO = p.sb('WO1', [128, 8, 1024], BF16)
            wok = load_w_bf16(p, WO, d1['w_o'], 'WO1')
            load_ln(p, st, d1['lng0'], d1['lnb0'])
            post_attn(p, st, WO, wok, X1, 'X1', Xmid, 'Xmid')
            p.barrier()
            p.es = es
        with ExitStack() as esb:
            p.es = esb
            alloc_peer(p, st)
            peer_layer(p, st, d1['wpq'], d1['subk'], uT_bf[1], v_bf[1], conv1, d1['lng1'], d1['lnb1'], Xmid, 'Xmid', out, 'out', False)
            p.barrier()
            p.es = es
        p.wait_all('sp', [('out', b) for b in range(NB)])
        print("build_F ops", p.nops, "waits", p.nwaits, flush=True)
    return nc


def _own_index(j):
    m = np.arange(NB)[:, None]
    r = np.arange(128)[None, :]
    return ((4 * m + j) * 128 + r).reshape(-1)


def _masks(j, kind):
    k = np.arange(128)[:, None]
    q = np.arange(128)[None, :]
    if kind == 'diff':
        diag = (k // 64) <= (q // 64)
    else:
        diag = k < q
    m = np.zeros((16, 128, 4, 128), np.float32)
    for u in range(16):
        for a in range(4):
            if u < 4 * a + j:
                m[u, :, a, :] = 1.0
            elif u == 4 * a + j:
                m[u, :, a, :] = diag
    return m.reshape(16, 128, 512).astype(ml_dtypes.bfloat16)


def _peer_inputs(layer, ln_g, ln_b, peer_w_q, peer_sub_keys, peer_u, peer_v, w_o):
    sk = np.ascontiguousarray(np.transpose(peer_sub_keys[layer], (1, 3, 0, 2)).reshape(128, 8, 128))
    return {
        'wpq': np.ascontiguousarray(peer_w_q[layer]),
        'subk': sk,
        'uT': np.ascontiguousarray(peer_u[layer].T),
        'v': np.ascontiguousarray(peer_v[layer]),
        'lng0': np.ascontiguousarray(ln_g[layer, 0][None, :]), 'lnb0': np.ascontiguousarray(ln_b[layer, 0][None, :]),
        'lng1': np.ascontiguousarray(ln_g[layer, 1][None, :]), 'lnb1': np.ascontiguousarray(ln_b[layer, 1][None, :]),
        'w_o': np.ascontiguousarray(w_o),
    }


def kernel(x, ln_g, ln_b, w_qkv_a, w_o_a, lambda_qk_a, subln_g_a, w_kv_b, w_q_b, w_o_b,
           peer_w_q, peer_sub_keys, peer_u, peer_v):
    f = lambda a: np.asarray(a, dtype=np.float32)
    x, ln_g, ln_b, w_qkv_a, w_o_a, lambda_qk_a, subln_g_a, w_kv_b, w_q_b, w_o_b, peer_w_q, peer_sub_keys, peer_u, peer_v = map(
        f, (x, ln_g, ln_b, w_qkv_a, w_o_a, lambda_qk_a, subln_g_a, w_kv_b, w_q_b, w_o_b, peer_w_q, peer_sub_keys, peer_u, peer_v))
    B = x.shape[0]
    perm = np.arange(D).reshape(16, 2, 32)[:, ::-1, :].reshape(-1)
    wq = np.ascontiguousarray(w_qkv_a[0][:, 0:D])
    wk = np.ascontiguousarray(w_qkv_a[0][:, D:2 * D])
    wv = np.ascontiguousarray(w_qkv_a[0][:, 2 * D:3 * D])
    wqr = np.ascontiguousarray(wq[:, perm])
    wkr = np.ascontiguousarray(wk[:, perm])
    dd_ = np.arange(128) % 64
    invf = (10000.0 ** (-(dd_ % 32).astype(np.float64) / 32.0)) / (2.0 * np.pi)
    sgn = np.where(dd_ < 32, -1.0, 1.0)
    ropec = np.stack([invf, sgn * TWO_PI_SAFE, np.full(128, TWO_PI_SAFE)], axis=1).astype(np.float32)
    pos_k = np.arange(S, dtype=np.float32)[None, :]
    xT = [np.ascontiguousarray(x[b].T) for b in range(B)]
    pa = _peer_inputs(0, ln_g, ln_b, peer_w_q, peer_sub_keys, peer_u, peer_v, w_o_a[0])
    common = {'pos_k': pos_k, 'ropec': ropec, 'w_q': wq, 'w_qr': wqr, 'w_k': wk, 'w_kr': wkr, 'w_v': wv,
              'lam': np.ascontiguousarray(lambda_qk_a[0].reshape(1, 256)), 'subg': np.ascontiguousarray(subln_g_a[0][None, :]),
              'w_kv': np.ascontiguousarray(w_kv_b)}
    common.update(pa)
    in_maps = []
    idxs = []
    for c in range(8):
        b, j = c // 4, c % 4
        idx = _own_index(j)
        idxs.append(idx)
        m = dict(common)
        m['x_own'] = np.ascontiguousarray(x[b, idx, :])
        m['xT'] = xT[b]
        m['pos_q'] = idx.astype(np.float32)[None, :]
        m['mask'] = _masks(j, 'diff')
        in_maps.append(m)
    if _DBG.get('stageA') is not None:
        if _DBG['stageA'] <= 3:
            in_maps = [{k: v for k, v in m.items() if k not in ('uT', 'v')} for m in in_maps]
        return run_bass_kernel_spmd(build_A(_DBG['stageA']), in_maps, core_ids=list(range(8))).results, in_maps
    pb = _peer_inputs(1, ln_g, ln_b, peer_w_q, peer_sub_keys, peer_u, peer_v, w_o_b[0])
    for c in range(8):
        m = in_maps[c]
        for k in [nm for nm, _ in PEER_IN]:
            m[k + '_0'] = m.pop(k)
            m[k + '_1'] = pb[k]
        m['w_qb'] = np.ascontiguousarray(w_q_b[0])
        m['mask_sb'] = _masks(c % 4, 'sb')
    ncF = build_F()
    resB = run_bass_kernel_spmd(ncF, in_maps, core_ids=list(range(8))).results
    out = np.zeros((B, S, D), np.float32)
    for c in range(8):
        out[c // 4, idxs[c], :] = np.asarray(resB[c]['out'], dtype=np.float32)
    return out


def build_P(real_tables):
    nc = bass.Bass("TRN2", target_bir_lowering=False)
    d = {}
    for nm, shp in [('x_own', (TQ, D))] + PEER_IN:
        if not real_tables and nm in ('uT', 'v'):
            d[nm] = _dscr(nc, nm, shp, F32)
            continue
        d[nm] = _din(nc, nm, shp)
    x1 = _dout(nc, 'x1', (TQ, D))
    dbg = _dout(nc, 'dbg', (128, 16), F32)
    dga = _dout(nc, 'dga', (128, 4096), BF16)
    uT_bf = _dscr(nc, 'uT_bf', (D, NEXP), BF16)
    v_bf = _dscr(nc, 'v_bf', (NEXP, D), BF16)
    with ExitStack() as es:
        p = Prog(nc, es)
        st = St()
        setup_common(p, st)
        make_xt(p, st, d['x_own'], 'x_own')
        convkeys = peer_convert(p, d['uT'], d['v'], uT_bf, v_bf, 'c0')
        alloc_peer(p, st)
        peer_layer(p, st, d['wpq'], d['subk'], uT_bf, v_bf, convkeys, d['lng1'], d['lnb1'], d['x_own'], 'x_own', x1, 'x1', True)
        p.barrier()
        p.dma('sp', dbg[:, 0:8], st.peer['BIAS'][:], writes=['dbg0'])
        p.dma('sp', dbg[:, 8:16], st.peer['GTH'][:], writes=['dbg1'])
        p.dma('sp', dga, st.peer['GA'][1][:], writes=['dga'])
        p.barrier()
        print("build_P ops", p.nops, "waits", p.nwaits, flush=True)
    return nc
```

```python
import math
from contextlib import ExitStack

import numpy as np
import ml_dtypes
import concourse.bass as bass
import concourse.mybir as mybir
from concourse.bass_utils import run_bass_kernel_spmd

F32 = mybir.dt.float32
BF16 = mybir.dt.bfloat16
I32 = mybir.dt.int32
AF = mybir.ActivationFunctionType
ALU = mybir.AluOpType
NDS = 40

D = 1024
S = 8192
NB = 16
TQ = 2048
ALPHA = 4.0 ** 0.25
LN_EPS = 1e-5
NEXP = 16384
TWO_PI_SAFE = 6.283185
_DBG = {}


class Prog:
    def __init__(self, nc, es):
        self.nc = nc
        self.es = es
        self.engs = {'pe': nc.tensor, 'act': nc.scalar, 'dve': nc.vector,
                     'pool': nc.gpsimd, 'sp': nc.sync}
        self.sem = {}
        self.cnt = {}
        for e in ['pe', 'act', 'dve', 'pool']:
            self.sem[e] = es.enter_context(nc.semaphore('s_' + e))
            self.cnt[e] = 0
        for i in range(NDS):
            self.sem['d%d' % i] = es.enter_context(nc.semaphore('sd%d' % i))
        self.dcnt = [0] * NDS
        self.dnext = 0
        self.seen = {e: {} for e in self.engs}
        self.bufs = {}
        self.know = {}
        self.nwaits = 0
        self.nops = 0

    def sb(self, name, shape, dt):
        self.uid = getattr(self, 'uid', 0) + 1
        return self.es.enter_context(self.nc.sbuf_tensor('sb%d_%s' % (self.uid, name), shape, dt))

    def ps(self, name, shape, dt):
        return self.es.enter_context(self.nc.psum_tensor(name, shape, dt))

    def _deps(self, reads, writes):
        deps = {}

        def add(k, v):
            if deps.get(k, 0) < v:
                deps[k] = v
        for b in reads:
            st = self.bufs.get(b)
            if st and st['w']:
                add(*st['w'])
        for b in writes:
            st = self.bufs.get(b)
            if st:
                if st['w']:
                    add(*st['w'])
                for k, v in st['r'].items():
                    add(k, v)
        return deps

    CE = ('pe', 'act', 'dve', 'pool')

    def _wait(self, e, deps):
        se = self.seen[e]
        for k, v in deps.items():
            if k == e and e == 'pe':
                continue
            if se.get(k, 0) < v:
                self.engs[e].wait_ge(self.sem[k], v)
                se[k] = v
                self.nwaits += 1
            kn = self.know.get((k, v))
            if kn is not None:
                for c, cv in zip(self.CE, kn):
                    if se.get(c, 0) < cv:
                        se[c] = cv

    def _snap(self, e, ev):
        se = self.seen[e]
        self.know[ev] = tuple(se.get(c, 0) for c in self.CE)

    def _record(self, ev, reads, writes):
        k, v = ev
        for b in reads:
            st = self.bufs.setdefault(b, {'w': None, 'r': {}})
            if st['r'].get(k, 0) < v:
                st['r'][k] = v
        for b in writes:
            self.bufs[b] = {'w': ev, 'r': {}}

    def op(self, e, fn, reads=(), writes=()):
        self._wait(e, self._deps(reads, writes))
        inst = fn(self.engs[e])
        self.cnt[e] += 1
        inst.then_inc(self.sem[e], 1)
        ev = (e, self.cnt[e])
        self._snap(e, ev)
        self._record(ev, reads, writes)
        self.nops += 1
        return ev

    def dma(self, q, out, in_, reads=(), writes=(), **kw):
        deps = self._deps(reads, writes)
        slot = self.dnext
        self.dnext = (slot + 1) % NDS
        k = 'd%d' % slot
        if self.dcnt[slot] > 0:
            deps[k] = max(deps.get(k, 0), 16 * self.dcnt[slot])
        self._wait(q, deps)
        inst = self.engs[q].dma_start(out=out, in_=in_, **kw)
        self.dcnt[slot] += 1
        inst.then_inc(self.sem[k], 16)
        ev = (k, 16 * self.dcnt[slot])
        self._snap(q, ev)
        self._record(ev, reads, writes)
        return ev

    def collective(self, kind, replica_groups, in_ap, out_ap, reads=(), writes=()):
        deps = self._deps(reads, writes)
        self._wait('pool', deps)
        self.ncc = getattr(self, 'ncc', 0) + 1
        k = 'cc%d' % self.ncc
        self.sem[k] = self.es.enter_context(self.nc.semaphore('s_' + k))
        inst = self.nc.gpsimd.collective_compute(kind, ALU.bypass, replica_groups=replica_groups,
                                                 ins=[in_ap.opt()], outs=[out_ap.opt()])
        inst.then_inc(self.sem[k], 1)
        ev = (k, 1)
        self._record(ev, reads, writes)
        return ev

    def wait_all(self, e, bufs):
        self._wait(e, self._deps(bufs, ()))

    def barrier(self):
        deps = {e: self.cnt[e] for e in ['pe', 'act', 'dve', 'pool'] if self.cnt[e] > 0}
        for i in range(NDS):
            if self.dcnt[i] > 0:
                deps['d%d' % i] = 16 * self.dcnt[i]
        for i in range(getattr(self, 'ncc', 0)):
            deps['cc%d' % (i + 1)] = 1
        for e in self.engs:
            self._wait(e, dict(deps))


class St:
    pass


def setup_common(p, st):
    st.PS = [p.ps('psb%d' % i, [128, 1024], F32) for i in range(4)]
    st.XT = p.sb('XT', [128, 8, TQ], BF16)
    st.identf = p.sb('identf', [128, 128], F32)
    st.ident = p.sb('ident', [128, 128], BF16)
    p.op('pool', lambda e: e.memset(st.identf[:], 1.0), writes=['identf'])
    p.op('pool', lambda e: e.affine_select(out=st.identf[:], in_=st.identf[:], pattern=[[-1, 128]],
                                           compare_op=ALU.is_equal, fill=0.0, base=0, channel_multiplier=1),
         reads=['identf'], writes=['identf'])
    p.op('dve', lambda e: e.tensor_copy(out=st.ident[:], in_=st.identf[:]), reads=['identf'], writes=['ident'])
    st.eps = p.sb('epsc', [128, 1], F32)
    p.op('pool', lambda e: e.memset(st.eps[:], LN_EPS), writes=['eps'])
    st.one = p.sb('onec', [128, 1], F32)
    p.op('pool', lambda e: e.memset(st.one[:], 1.0), writes=['one'])
    st.lnw = p.sb('lnw', [128, 2, D], F32)
    st.ysb = p.sb('ysb', [128, D], F32)
    st.xin = [p.sb('xin%d' % i, [128, D], F32) for i in range(2)]
    st.xo = [p.sb('xo%d' % i, [128, D], F32) for i in range(2)]
    st.lnst = p.sb('lnst', [128, 2, 6], F32)
    st.lnmv = p.sb('lnmv', [128, 4], F32)


def bank(st, i):
    return st.PS[i // 2][:, (i % 2) * 512:(i % 2) * 512 + 512]


def xt_from_tile(p, st, blk, xt_tile, xt_key):
    for half in range(2):
        bk = 6 + half
        pst = bank(st, bk)
        for i in range(4):
            dc = half * 4 + i
            p.op('pe', lambda e: e.transpose(out=pst[:, i * 128:(i + 1) * 128], in_=xt_tile[:, dc * 128:(dc + 1) * 128],
                                             identity=st.identf[:]),
                 reads=[xt_key, 'identf'], writes=[('bank', bk)])
        outap = st.XT[:, half * 4:half * 4 + 4, blk * 128:(blk + 1) * 128]
        inap = pst.rearrange("p (a b) -> p a b", a=4)
        if half == 0:
            p.op('act', lambda e: e.copy(out=outap, in_=inap), reads=[('bank', bk)], writes=[('XT', blk, half)])
        else:
            p.op('dve', lambda e: e.tensor_copy(out=outap, in_=inap), reads=[('bank', bk)], writes=[('XT', blk, half)])


def make_xt(p, st, x_dram, xkey):
    for blk in range(NB):
        t = st.xin[blk % 2]
        p.dma('sp', t[:], x_dram[blk * 128:(blk + 1) * 128, :], reads=[(xkey, blk)], writes=[('xin', blk % 2)])
        xt_from_tile(p, st, blk, t, ('xin', blk % 2))


def xt_keys(blks):
    return [('XT', b, h) for b in blks for h in range(2)]


def load_ln(p, st, g_dram, b_dram):
    p.dma('sp', st.lnw[:, 0, :], g_dram[0:1, :].to_broadcast([128, D]), writes=['lng'])
    p.dma('sp', st.lnw[:, 1, :], b_dram[0:1, :].to_broadcast([128, D]), writes=['lnb'])


def finish_block(p, st, blk, ps_tile, ps_keys, x_src, skey, x_dst, dkey, do_xt=True):
    xi = st.xin[blk % 2]
    xo = st.xo[blk % 2]
    p.dma('sp', xi[:], x_src[blk * 128:(blk + 1) * 128, :], reads=[(skey, blk)], writes=[('xin', blk % 2)])
    p.op('dve', lambda e: e.scalar_tensor_tensor(out=st.ysb[:], in0=xi[:], scalar=ALPHA, op0=ALU.mult,
                                                 in1=ps_tile, op1=ALU.add),
         reads=[('xin', blk % 2)] + ps_keys, writes=['ysb'])
    for i in range(2):
        p.op('dve', lambda e: e.bn_stats(out=st.lnst[:, i, :], in_=st.ysb[:, i * 512:(i + 1) * 512]),
             reads=['ysb'], writes=[('lnst', i)])
    p.op('dve', lambda e: e.bn_aggr(out=st.lnmv[:, 0:2], in_=st.lnst[:].rearrange("p a b -> p (a b)")),
         reads=[('lnst', 0), ('lnst', 1)], writes=['lnmv'])
    p.op('act', lambda e: e.activation(out=st.lnmv[:, 2:3], in_=st.lnmv[:, 1:2], func=AF.Ln, bias=st.eps[:], scale=1.0),
         reads=['lnmv', 'eps'], writes=['lnmv2'])
    p.op('act', lambda e: e.activation(out=st.lnmv[:, 3:4], in_=st.lnmv[:, 2:3], func=AF.Exp, scale=-0.5),
         reads=['lnmv2'], writes=['lnmv3'])
    p.op('dve', lambda e: e.tensor_scalar(out=st.ysb[:], in0=st.ysb[:], scalar1=st.lnmv[:, 0:1], scalar2=st.lnmv[:, 3:4],
                                          op0=ALU.subtract, op1=ALU.mult),
         reads=['ysb', 'lnmv', 'lnmv3'], writes=['ysb'])
    p.op('pool', lambda e: e.tensor_tensor(out=st.ysb[:], in0=st.ysb[:], in1=st.lnw[:, 0, :], op=ALU.mult),
         reads=['ysb', 'lng'], writes=['ysb'])
    p.op('pool', lambda e: e.tensor_tensor(out=xo[:], in0=st.ysb[:], in1=st.lnw[:, 1, :], op=ALU.add),
         reads=['ysb', 'lnb'], writes=[('xo', blk % 2)])
    p.dma('sp', x_dst[blk * 128:(blk + 1) * 128, :], xo[:], reads=[('xo', blk % 2)], writes=[(dkey, blk)])
    if do_xt:
        xt_from_tile(p, st, blk, xo, ('xo', blk % 2))


def load_w_bf16(p, dst, w_dram, key, ncols=1024, col0=0):
    src = w_dram.rearrange("(c p) n -> p c n", p=128)[:, :, col0:col0 + ncols]
    for c0 in range(0, 8, 2):
        p.dma('pool', dst[:, c0:c0 + 2, :], src[:, c0:c0 + 2, :], writes=[(key, c0)])
    return [(key, c0) for c0 in range(0, 8, 2)]


def rope_tables(p, st, pos_dram, n, scale):
    R = st.rope
    p.dma('sp', R['pos'][:, 0:n], pos_dram.to_broadcast([128, n]), writes=['rpos'])
    p.op('dve', lambda e: e.tensor_scalar(out=R['y'][:, 0:n], in0=R['pos'][:, 0:n], scalar1=st.ropec[:, 0:1],
                                          scalar2=None, op0=ALU.mult), reads=['rpos', 'ropec'], writes=['ry'])
    for which in range(2):
        y = R['y'][:, 0:n]
        r = R['r'][:, 0:n]
        if which == 0:
            p.op('dve', lambda e: e.tensor_scalar(out=R['y2'][:, 0:n], in0=y, scalar1=0.25, scalar2=None, op0=ALU.add),
                 reads=['ry'], writes=['ry2'])
            y = R['y2'][:, 0:n]
            ykey = 'ry2'
        else:
            ykey = 'ry'
        p.op('dve', lambda e: e.tensor_copy(out=R['yi'][:, 0:n], in_=y), reads=[ykey], writes=['ryi'])
        p.op('dve', lambda e: e.tensor_copy(out=R['yf'][:, 0:n], in_=R['yi'][:, 0:n]), reads=['ryi'], writes=['ryf'])
        p.op('dve', lambda e: e.tensor_tensor(out=r, in0=y, in1=R['yf'][:, 0:n], op=ALU.subtract),
             reads=[ykey, 'ryf'], writes=['rr'])
        p.op('dve', lambda e: e.tensor_scalar(out=R['yf'][:, 0:n], in0=r, scalar1=0.5, scalar2=None, op0=ALU.is_gt),
             reads=['rr'], writes=['ryf'])
        p.op('dve', lambda e: e.tensor_tensor(out=r, in0=r, in1=R['yf'][:, 0:n], op=ALU.subtract),
             reads=['rr', 'ryf'], writes=['rr'])
        p.op('dve', lambda e: e.tensor_scalar(out=R['yf'][:, 0:n], in0=r, scalar1=-0.5, scalar2=None, op0=ALU.is_lt),
             reads=['rr'], writes=['ryf'])
        p.op('dve', lambda e: e.tensor_tensor(out=r, in0=r, in1=R['yf'][:, 0:n], op=ALU.add),
             reads=['rr', 'ryf'], writes=['rr'])
        tab = R['cos'] if which == 0 else R['sin']
        tkey = 'rcos' if which == 0 else 'rsin'
        sc = st.ropec[:, 2:3] if which == 0 else st.ropec[:, 1:2]
        p.op('act', lambda e: e.activation(out=tab[:, 0:n], in_=r, func=AF.Sin, scale=sc),
             reads=['rr', 'ropec'], writes=[tkey])
        if scale != 1.0:
            p.op('dve', lambda e: e.tensor_scalar(out=tab[:, 0:n], in0=tab[:, 0:n], scalar1=scale, scalar2=None, op0=ALU.mult),
                 reads=[tkey], writes=[tkey])


def proj_rope(p, st, W, WR, wkeys, wrkeys, rhs_fn, rhs_keys, n, out_fn, out_keys_fn):
    R = st.rope
    for hc in range(8):
        b0, b1 = (hc % 2) * 2, (hc % 2) * 2 + 1
        for (Wt, wk, bk) in ((W, wkeys, b0), (WR, wrkeys, b1)):
            for dc in range(8):
                p.op('pe', lambda e: e.matmul(bank(st, bk)[:, 0:n], lhsT=Wt[:, dc, hc * 128:(hc + 1) * 128], rhs=rhs_fn(dc),
                                              start=(dc == 0), stop=(dc == 7)),
                     reads=wk + rhs_keys, writes=[('bank', bk)])
        t1 = R['t1'][hc % 2]
        t2 = R['t2'][hc % 2]
        p.op('dve', lambda e: e.tensor_tensor(out=t1[:, 0:n], in0=bank(st, b0)[:, 0:n], in1=R['cos'][:, 0:n], op=ALU.mult),
             reads=[('bank', b0), 'rcos'], writes=[('rt1', hc % 2)])
        p.op('dve', lambda e: e.tensor_tensor(out=t2[:, 0:n], in0=bank(st, b1)[:, 0:n], in1=R['sin'][:, 0:n], op=ALU.mult),
             reads=[('bank', b1), 'rsin'], writes=[('rt2', hc % 2)])
        p.op('pool', lambda e: e.tensor_tensor(out=out_fn(hc), in0=t1[:, 0:n], in1=t2[:, 0:n], op=ALU.add),
             reads=[('rt1', hc % 2), ('rt2', hc % 2)], writes=out_keys_fn(hc))


def group_tiles(g, descending):
    tiles = [(kb, None, 0) for kb in range(16 * g)]
    tiles += [(16 * g + u, u, u // 4) for u in range(16)]
    if descending:
        tiles = tiles[::-1]
    return tiles


def post_attn(p, st, WO, wokeys, x_src, skey, x_dst, dkey):
    OT = st.OT
    for blk in range(NB):
        pst = bank(st, 4).bitcast(BF16)
        for h in range(8):
            p.op('pe', lambda e: e.transpose(out=pst[:, h * 128:(h + 1) * 128], in_=st.OALL[:, blk, h * 128:(h + 1) * 128],
                                             identity=st.ident[:]),
                 reads=[('OALL', blk), 'ident'], writes=[('bank', 4)])
        p.op('act', lambda e: e.copy(out=OT[:].rearrange("p a b -> p (a b)"), in_=pst), reads=[('bank', 4)], writes=['OT'])
        for half in range(2):
            for h in range(8):
                p.op('pe', lambda e: e.matmul(bank(st, half), lhsT=OT[:, h, :], rhs=WO[:, h, half * 512:(half + 1) * 512],
                                              start=(h == 0), stop=(h == 7)),
                     reads=['OT'] + wokeys, writes=[('bank', half)])
        finish_block(p, st, blk, st.PS[0][:], [('bank', 0), ('bank', 1)], x_src, skey, x_dst, dkey)


def peer_convert(p, uT_dram, v_dram, uT_bf, v_bf, tag):
    keys = []
    us = uT_dram.rearrange("r (a b) -> (r a) b", b=2048)
    ud = uT_bf.rearrange("r (a b) -> (r a) b", b=2048)
    for i in range(8):
        p.dma('pool', ud[i * 1024:(i + 1) * 1024, :], us[i * 1024:(i + 1) * 1024, :], writes=[(tag + 'u', i)])
        keys.append((tag + 'u', i))
    for i in range(16):
        p.dma('pool', v_bf[i * 1024:(i + 1) * 1024, :], v_dram[i * 1024:(i + 1) * 1024, :], writes=[(tag + 'v', i)])
        keys.append((tag + 'v', i))
    return keys


def peer_layer(p, st, wpq_dram, subk_dram, uT_bf, v_bf, convkeys, g_dram, b_dram, x_src, skey, x_dst, dkey, do_xt):
    PE = st.peer
    load_ln(p, st, g_dram, b_dram)
    wpqk = load_w_bf16(p, PE['WPQ'], wpq_dram, 'WPQ')
    p.dma('pool', PE['SUBK'][:], subk_dram, writes=['SUBK'])
    p.op('pool', lambda e: e.memset(PE['QP'][:], 0.0), writes=['QP'])
    p.op('pool', lambda e: e.memset(PE['QPH'][:], 0.0), writes=['QPH'])
    uview = uT_bf.rearrange("(c p) e -> p c e", p=128)
    vview = v_bf.rearrange("(t i p) d -> t p i d", p=128, i=4)
    NT = NEXP // 512
    for blk in range(_DBG.get('peer_blocks', NB)):
        tok = slice(blk * 128, (blk + 1) * 128)
        xtk = xt_keys([blk])
        for h in range(8):
            bk = 4 + h // 4
            for dc in range(8):
                p.op('pe', lambda e: e.matmul(bank(st, bk)[:, (h % 4) * 128:(h % 4 + 1) * 128],
                                              lhsT=PE['WPQ'][:, dc, h * 128:(h + 1) * 128], rhs=st.XT[:, dc, tok],
                                              start=(dc == 0), stop=(dc == 7)),
                     reads=wpqk + xtk, writes=[('bank', bk)])
        p.op('act', lambda e: e.copy(out=PE['QP'][0:64, :, :].rearrange("p a b -> p (a b)"), in_=st.PS[2][0:64, :]),
             reads=[('bank', 4), ('bank', 5)], writes=['QP'])
        p.op('act', lambda e: e.copy(out=PE['QPH'][64:128, :, :].rearrange("p a b -> p (a b)"), in_=st.PS[2][64:128, :]),
             reads=[('bank', 4), ('bank', 5)], writes=['QPH'])
        for h in range(8):
            for c in range(2):
                j = h * 2 + c
                bk = 4 + j // 4
                p.op('pe', lambda e: e.matmul(bank(st, bk)[:, (j % 4) * 128:(j % 4 + 1) * 128],
                                              lhsT=(PE['QP'] if c == 0 else PE['QPH'])[:, h, :], rhs=PE['SUBK'][:, h, :],
                                              start=True, stop=True),
                     reads=['QP', 'QPH', 'SUBK'], writes=[('bank', bk)])
        for h in range(8):
            T = PE['T24']
            for c in range(2):
                j = h * 2 + c
                src = bank(st, 4 + j // 4)[:, (j % 4) * 128:(j % 4 + 1) * 128]
                sk = ('bank', 4 + j // 4)
                W1 = PE['W1']
                p.op('dve', lambda e: e.max(out=T[:, c, 0:8], in_=src), reads=[sk], writes=[('T24', c)])
                p.op('dve', lambda e: e.match_replace(out=W1[:], in_to_replace=T[:, c, 0:8], in_values=src, imm_value=-1e30),
                     reads=[sk, ('T24', c)], writes=['W1'])
                p.op('dve', lambda e: e.max(out=T[:, c, 8:16], in_=W1[:]), reads=['W1'], writes=[('T24', c)])
                p.op('dve', lambda e: e.match_replace(out=W1[:], in_to_replace=T[:, c, 8:16], in_values=W1[:], imm_value=-1e30),
                     reads=['W1', ('T24', c)], writes=['W1'])
                p.op('dve', lambda e: e.max(out=T[:, c, 16:24], in_=W1[:]), reads=['W1'], writes=[('T24', c)])
            W2 = PE['W2']
            p.op('dve', lambda e: e.tensor_tensor(out=W2[:], in0=T[:, 0, :].unsqueeze(2).to_broadcast([128, 24, 24]),
                                                  in1=T[:, 1, :].unsqueeze(1).to_broadcast([128, 24, 24]), op=ALU.add),
                 reads=[('T24', 0), ('T24', 1)], writes=['W2'])
            C = PE['C24']
            W2f = W2[:].rearrange("p a b -> p (a b)")
            p.op('dve', lambda e: e.max(out=C[:, 0:8], in_=W2f), reads=['W2'], writes=['C24'])
            p.op('dve', lambda e: e.match_replace(out=W2f, in_to_replace=C[:, 0:8], in_values=W2f, imm_value=-1e30),
                 reads=['W2', 'C24'], writes=['W2'])
            p.op('dve', lambda e: e.max(out=C[:, 8:16], in_=W2f), reads=['W2'], writes=['C24'])
            p.op('dve', lambda e: e.match_replace(out=W2f, in_to_replace=C[:, 8:16], in_values=W2f, imm_value=-1e30),
                 reads=['W2', 'C24'], writes=['W2'])
            p.op('dve', lambda e: e.max(out=C[:, 16:24], in_=W2f), reads=['W2'], writes=['C24'])
            sm = PE['SM']
            p.op('dve', lambda e: e.tensor_scalar(out=sm[:, 0:1], in0=C[:, 0:1], scalar1=-1.0, scalar2=None, op0=ALU.mult),
                 reads=['C24'], writes=['SM0'])
            p.op('act', lambda e: e.activation(out=PE['EX16'][:], in_=C[:, 0:16], func=AF.Exp, bias=sm[:, 0:1], scale=1.0,
                                               accum_out=sm[:, 1:2]),
                 reads=['C24', 'SM0'], writes=['EX16', 'SM1'])
            p.op('act', lambda e: e.activation(out=sm[:, 2:3], in_=sm[:, 1:2], func=AF.Ln), reads=['SM1'], writes=['SM2'])
            p.op('dve', lambda e: e.tensor_tensor(out=PE['BIAS'][:, h:h + 1], in0=sm[:, 0:1], in1=sm[:, 2:3], op=ALU.subtract),
                 reads=['SM0', 'SM2'], writes=[('BIAS', h)])
            p.op('dve', lambda e: e.tensor_tensor(out=sm[:, 3:4], in0=C[:, 15:16], in1=C[:, 16:17], op=ALU.add),
                 reads=['C24'], writes=['SM3'])
            p.op('dve', lambda e: e.tensor_scalar(out=PE['GTH'][:, h:h + 1], in0=sm[:, 3:4], scalar1=0.5, scalar2=None, op0=ALU.mult),
                 reads=['SM3'], writes=[('GTH', h)])
        if _DBG.get('peer_sub', 9) <= 1:
            continue
        def gates_head(k, h):
            GS = PE['GS'][k % 2]
            cb = 2 + (h % 4)
            p.op('pe', lambda e: e.matmul(bank(st, cb).rearrange("p (a b) -> p a b", a=4), lhsT=PE['QP'][:, h, :],
                                          rhs=PE['SUBK'][:, h, k * 4:k * 4 + 4].unsqueeze(2).to_broadcast([128, 4, 128]),
                                          start=True, stop=False),
                 reads=['QP', 'QPH', 'SUBK'], writes=[('bank', cb)])
            p.op('pe', lambda e: e.matmul(bank(st, cb).rearrange("p (a b) -> p a b", a=4), lhsT=PE['QPH'][:, h, :],
                                          rhs=PE['SUBK'][:, h, :].unsqueeze(1).to_broadcast([128, 4, 128]),
                                          start=False, stop=True),
                 reads=['QP', 'QPH', 'SUBK'], writes=[('bank', cb)])
            EF = PE['EF'][h % 4]
            p.op('act', lambda e: e.activation(out=EF[:], in_=bank(st, cb), func=AF.Exp, bias=PE['BIAS'][:, h:h + 1], scale=1.0),
                 reads=[('bank', cb), ('BIAS', h)], writes=[('EF', h % 4)])
            p.op('dve', lambda e: e.scalar_tensor_tensor(out=GS[:, h, :], in0=bank(st, cb), scalar=PE['GTH'][:, h:h + 1], op0=ALU.is_ge,
                                                         in1=EF[:], op1=ALU.mult),
                 reads=[('bank', cb), ('EF', h % 4), ('GTH', h)], writes=[('GS', k % 2, h)])
            if h in (1, 3, 5):
                p.op('pool', lambda e: e.tensor_tensor(out=GS[:, h - 1, :], in0=GS[:, h - 1, :], in1=GS[:, h, :], op=ALU.add),
                     reads=[('GS', k % 2, h - 1), ('GS', k % 2, h)], writes=[('GS', k % 2, h - 1)])

        def gates_tail(k):
            GS = PE['GS'][k % 2]
            gk = lambda h: ('GS', k % 2, h)
            p.op('dve', lambda e: e.tensor_tensor(out=GS[:, 6, :], in0=GS[:, 6, :], in1=GS[:, 7, :], op=ALU.add),
                 reads=[gk(6), gk(7)], writes=[gk(6)])
            p.op('dve', lambda e: e.tensor_tensor(out=GS[:, 0, :], in0=GS[:, 0, :], in1=GS[:, 2, :], op=ALU.add),
                 reads=[gk(0), gk(2)], writes=[gk(0)])
            p.op('dve', lambda e: e.tensor_tensor(out=GS[:, 4, :], in0=GS[:, 4, :], in1=GS[:, 6, :], op=ALU.add),
                 reads=[gk(4), gk(6)], writes=[gk(4)])
            p.op('dve', lambda e: e.tensor_tensor(out=PE['GA'][k % 3][:, 0:512], in0=GS[:, 0, :], in1=GS[:, 4, :], op=ALU.add),
                 reads=[gk(0), gk(4)], writes=[('GA', k % 3)])

        def expert_loads(k):
            ub = k % 3
            p.dma('sp', PE['UT'][ub][:], uview[:, :, k * 512:(k + 1) * 512], reads=convkeys, writes=[('UT', ub)])
            p.dma('sp', PE['VT'][k % 4][:], vview[k], reads=convkeys, writes=[('VT', k % 4)])

        def expert_chunk_a(k, c):
            ub = k % 4
            UT = PE['UT'][k % 3]
            GL = PE['GL'][k % 2]
            GH = PE['GH'][k % 2]
            if c < 4:
                for dc in (2 * c, 2 * c + 1):
                    p.op('pe', lambda e: e.matmul(bank(st, 6), lhsT=st.XT[:, dc, tok], rhs=UT[:, dc, :], start=(dc == 0), stop=(dc == 7)),
                         reads=xtk + [('UT', k % 3)], writes=[('bank', 6)])
            if c == 3:
                p.op('act', lambda e: e.activation(out=GL[:], in_=bank(st, 6), func=AF.Gelu), reads=[('bank', 6)], writes=[('GL', k % 2)])
            if c == 5:
                p.op('dve', lambda e: e.tensor_tensor(out=GH[:], in0=GL[:], in1=PE['GA'][k % 3][:, 0:512], op=ALU.mult),
                     reads=[('GL', k % 2), ('GA', k % 3)], writes=[('GH', k % 2)])
            if c == 7:
                ptr = bank(st, 7).bitcast(BF16)
                for i in range(4):
                    p.op('pe', lambda e: e.transpose(out=ptr[:, i * 128:(i + 1) * 128], in_=GH[:, i * 128:(i + 1) * 128], identity=st.ident[:]),
                         reads=[('GH', k % 2), 'ident'], writes=[('bank', 7)])

        def expert_chunk_b(k, c):
            VT = PE['VT'][k % 4]
            GHT = PE['GHT'][k % 2]
            if c == 1:
                ptr = bank(st, 7).bitcast(BF16)
                p.op('dve', lambda e: e.tensor_copy(out=GHT[:].rearrange("p a b -> p (a b)"), in_=ptr[:, 0:512]),
                     reads=[('bank', 7)], writes=[('GHT', k % 2)])
            if c in (2, 3):
                half = c - 2
                for i in range(4):
                    p.op('pe', lambda e: e.matmul(bank(st, half), lhsT=GHT[:, i, :], rhs=VT[:, i, half * 512:(half + 1) * 512],
                                                  start=(k == 0 and i == 0), stop=(k == NT - 1 and i == 3)),
                         reads=[('GHT', k % 2), ('VT', k % 4)], writes=[('bank', half)])

        for k in range(NT + 2):
            if k < NT:
                expert_loads(k)
            for h in range(8):
                if k < NT:
                    gates_head(k, h)
                if 1 <= k <= NT:
                    expert_chunk_a(k - 1, h)
                if k >= 2:
                    expert_chunk_b(k - 2, h)
            if k < NT:
                gates_tail(k)
        if _DBG.get('peer_sub', 9) <= 3:
            continue
        finish_block(p, st, blk, st.PS[0][:], [('bank', 0), ('bank', 1)], x_src, skey, x_dst, dkey, do_xt)


def alloc_peer(p, st):
    PE = {}
    PE['WPQ'] = p.sb('WPQ', [128, 8, 1024], BF16)
    PE['SUBK'] = p.sb('SUBK', [128, 8, 128], BF16)
    PE['QP'] = p.sb('QP', [128, 8, 128], BF16)
    PE['QPH'] = p.sb('QPH', [128, 8, 128], BF16)
    PE['T24'] = p.sb('T24', [128, 2, 24], F32)
    PE['W1'] = p.sb('W1', [128, 128], F32)
    PE['W2'] = p.sb('W2', [128, 24, 24], F32)
    PE['C24'] = p.sb('C24', [128, 24], F32)
    PE['SM'] = p.sb('SM', [128, 8], F32)
    PE['EX16'] = p.sb('EX16', [128, 16], F32)
    PE['BIAS'] = p.sb('BIAS', [128, 8], F32)
    PE['GTH'] = p.sb('GTH', [128, 8], F32)
    PE['GA'] = [p.sb('GA%d' % i, [128, 512], BF16) for i in range(3)]
    PE['EF'] = [p.sb('EF%d' % i, [128, 512], BF16) for i in range(4)]
    PE['GS'] = [p.sb('GS%d' % i, [128, 8, 512], BF16) for i in range(2)]
    PE['UT'] = [p.sb('UT%d' % i, [128, 8, 512], BF16) for i in range(3)]
    PE['VT'] = [p.sb('VT%d' % i, [128, 4, 1024], BF16) for i in range(4)]
    PE['GL'] = [p.sb('GL%d' % i, [128, 512], BF16) for i in range(2)]
    PE['GH'] = [p.sb('GH%d' % i, [128, 512], BF16) for i in range(2)]
    PE['GHT'] = [p.sb('GHT%d' % i, [128, 4, 128], BF16) for i in range(2)]
    st.peer = PE


def alloc_rope(p, st):
    R = {}
    for nm in ['pos', 'y', 'y2', 'yf', 'r', 'cos', 'sin']:
        R[nm] = p.sb('rp_' + nm, [128, 512], F32)
    R['yi'] = p.sb('rp_yi', [128, 512], I32)
    R['t1'] = [p.sb('rp_t1%d' % i, [128, 512], F32) for i in range(2)]
    R['t2'] = [p.sb('rp_t2%d' % i, [128, 512], F32) for i in range(2)]
    st.rope = R


def diff_projections(p, st, d, KTs, Vs):
    with ExitStack() as es2:
        old = p.es
        p.es = es2
        alloc_rope(p, st)
        W0 = p.sb('W0', [128, 8, 1024], BF16)
        W1 = p.sb('W1p', [128, 8, 1024], BF16)
        W2 = p.sb('W2p', [128, 8, 1024], BF16)
        XK = [p.sb('XK%d' % i, [128, 8, 512], BF16) for i in range(2)]
        KTt = [p.sb('KTt%d' % i, [128, 8, 512], BF16) for i in range(2)]
        Vt = [p.sb('Vt0', [128, 4, 1024], BF16)] * 2
        k0 = load_w_bf16(p, W0, d['w_q'], 'W0')
        k1 = load_w_bf16(p, W1, d['w_qr'], 'W1')
        for tcn in range(4):
            sl = slice(tcn * 512, (tcn + 1) * 512)
            rope_tables(p, st, d['pos_q'][0:1, sl], 512, 0.125)
            proj_rope(p, st, W0, W1, k0, k1, lambda dc: st.XT[:, dc, sl], xt_keys(range(tcn * 4, tcn * 4 + 4)), 512,
                      lambda hc: st.QT[:, hc, sl], lambda hc: [('QT', hc, tcn)])
        k0 = load_w_bf16(p, W0, d['w_k'], 'W0')
        k1 = load_w_bf16(p, W1, d['w_kr'], 'W1')
        k2 = load_w_bf16(p, W2, d['w_v'], 'W2')
        xTv = d['xT'].rearrange("(c p) t -> p c t", p=128)
        KTv = KTs.rearrange("(c p) t -> p c t", p=128)
        Vv = Vs.rearrange("(b p) n -> p b n", p=128)
        for tcn in range(16):
            sl = slice(tcn * 512, (tcn + 1) * 512)
            xk = XK[tcn % 2]
            xkk = ('XK', tcn % 2)
            for c0 in range(0, 8, 4):
                p.dma('pool', xk[:, c0:c0 + 4, :], xTv[:, c0:c0 + 4, sl], writes=[(xkk, c0)])
            xkeys = [(xkk, 0), (xkk, 4)]
            rope_tables(p, st, d['pos_k'][0:1, sl], 512, 1.0)
            kt = KTt[tcn % 2]
            proj_rope(p, st, W0, W1, k0, k1, lambda dc: xk[:, dc, :], xkeys, 512,
                      lambda hc: kt[:, hc, :], lambda hc: [('KTt', tcn % 2, hc)])
            p.dma('sp', KTv[:, :, sl], kt[:], reads=[('KTt', tcn % 2, hc) for hc in range(8)], writes=[('KTs', tcn)])
            vt = Vt[tcn % 2]
            for tb in range(4):
                for half in range(2):
                    bk = 4 + (tb * 2 + half) % 4
                    for dc in range(8):
                        p.op('pe', lambda e: e.matmul(bank(st, bk), lhsT=xk[:, dc, tb * 128:(tb + 1) * 128],
                                                      rhs=W2[:, dc, half * 512:(half + 1) * 512], start=(dc == 0), stop=(dc == 7)),
                             reads=xkeys + k2, writes=[('bank', bk)])
                    p.op('act', lambda e: e.copy(out=vt[:, tb, half * 512:(half + 1) * 512], in_=bank(st, bk)),
                         reads=[('bank', bk)], writes=[('Vt', 0, tb, half)])
            p.dma('sp', Vv[:, tcn * 4:(tcn + 1) * 4, :], vt[:],
                  reads=[('Vt', 0, tb, half) for tb in range(4) for half in range(2)], writes=[('Vs', tcn)])
        p.barrier()
        p.es = old


def attn_diff(p, st, d, KTs, Vs):
    with ExitStack() as es2:
        old = p.es
        p.es = es2
        KT = p.sb('KTh', [128, S], BF16)
        VH = p.sb('VHh', [128, 64, 132], BF16)
        MK = p.sb('MK', [128, 16, 512], BF16)
        E = [p.sb('E%d' % i, [128, 512], BF16) for i in range(4)]
        lam = p.sb('lam', [128, 4, 64], F32)
        lsm = p.sb('lsm', [128, 8], F32)
        gsc = p.sb('gsc', [128, 128], F32)
        ep = p.sb('ep', [128, 8], F32)
        d0 = p.sb('d0', [128, 128], F32)
        dd = p.sb('dd', [128, 128], F32)
        junk = p.sb('junk', [128, 128], F32)
        QZ = [p.sb('QZd%d' % i, [128, TQ], BF16) for i in range(2)]
        for i in range(2):
            p.op('pool', lambda e: e.memset(QZ[i][:], 0.0), writes=[('QZ', i)])
        p.dma('sp', MK[:], d['mask'].rearrange("u k q -> k u q"), writes=['MK'])
        p.op('pool', lambda e: e.memset(VH[:, :, 128:132], 1.0), writes=['VHones'])
        p.dma('sp', lam[:].rearrange("p a b -> p (a b)"), d['lam'][0:1, :].to_broadcast([128, 256]), writes=['lam'])
        p.dma('sp', gsc[:], d['subg'][0:1, :].to_broadcast([128, 128]), writes=['gsc'])
        lam_init = 0.8 - 0.6 * math.exp(0.0)
        for i in range(2):
            p.op('dve', lambda e: e.tensor_tensor(out=junk[:, 0:64], in0=lam[:, 2 * i, :], in1=lam[:, 2 * i + 1, :], op=ALU.mult),
                 reads=['lam'], writes=['junk'])
            p.op('dve', lambda e: e.tensor_reduce(out=lsm[:, i:i + 1], in_=junk[:, 0:64], op=ALU.add, axis=mybir.AxisListType.X),
                 reads=['junk'], writes=[('lsm', i)])
            p.op('act', lambda e: e.activation(out=lsm[:, 2 + i:3 + i], in_=lsm[:, i:i + 1], func=AF.Exp),
                 reads=[('lsm', i)], writes=[('lsm', 2 + i)])
        p.op('dve', lambda e: e.tensor_tensor(out=lsm[:, 4:5], in0=lsm[:, 3:4], in1=lsm[:, 2:3], op=ALU.subtract),
             reads=[('lsm', 2), ('lsm', 3)], writes=[('lsm', 4)])
        p.op('dve', lambda e: e.tensor_scalar(out=lsm[:, 5:6], in0=lsm[:, 4:5], scalar1=-lam_init, scalar2=None, op0=ALU.add),
             reads=[('lsm', 4)], writes=['neglam'])
        p.op('dve', lambda e: e.tensor_scalar(out=gsc[:], in0=gsc[:], scalar1=1.0 - lam_init, scalar2=None, op0=ALU.mult),
             reads=['gsc'], writes=['gsc'])
        Vv = Vs.rearrange("(b p) n -> p b n", p=128)
        ei = 0
        for h in range(8):
            for q4 in range(4):
                p.dma('sp', KT[:, q4 * 2048:(q4 + 1) * 2048], KTs[h * 128:(h + 1) * 128, q4 * 2048:(q4 + 1) * 2048],
                      reads=[('KTs', t) for t in range(q4 * 4, q4 * 4 + 4)], writes=[('KT', q4)])
                p.dma('sp', VH[:, q4 * 16:(q4 + 1) * 16, 0:128], Vv[:, q4 * 16:(q4 + 1) * 16, h * 128:(h + 1) * 128],
                      reads=[('Vs', t) for t in range(q4 * 4, q4 * 4 + 4)], writes=[('VH', q4)])
            for mp_ in range(2):
                p.op('pool', lambda e: e.tensor_copy(out=QZ[mp_][mp_ * 64:(mp_ + 1) * 64, :], in_=st.QT[mp_ * 64:(mp_ + 1) * 64, h, :]),
                     reads=[('QT', h, t) for t in range(4)], writes=[('QZ', mp_)])
            for g in range(4):
                tiles = []
                for (kb, u, amin) in group_tiles(g, False):
                    for mp in range(2):
                        tiles.append((kb, u, amin, mp))
                nt = len(tiles)
                lastkb = [16 * g + 4 * a + 3 for a in range(4)]
                slots = {}

                def stage1(i):
                    kb, u, amin, mp = tiles[i]
                    n = 512 - 128 * amin
                    sb_ = 4 + (i % 4)
                    slots[i] = sb_
                    p.op('pe', lambda e: e.matmul(bank(st, sb_)[:, 0:n], lhsT=KT[:, kb * 128:(kb + 1) * 128],
                                                  rhs=QZ[mp][:, g * 512 + amin * 128:(g + 1) * 512],
                                                  start=True, stop=True),
                         reads=[('KT', kb // 16), ('QZ', mp)], writes=[('bank', sb_)])

                def stage2(i):
                    kb, u, amin, mp = tiles[i]
                    n = 512 - 128 * amin
                    sb_ = slots[i]
                    Et = E[i % 4]
                    p.op('act', lambda e: e.activation(out=Et[:, 0:n], in_=bank(st, sb_)[:, 0:n], func=AF.Exp),
                         reads=[('bank', sb_)], writes=[('E', i % 4)])
                    if u is not None:
                        p.op('pool', lambda e: e.tensor_tensor(out=Et[:, 0:n], in0=Et[:, 0:n], in1=MK[:, u, amin * 128:512], op=ALU.mult),
                             reads=[('E', i % 4), 'MK'], writes=[('E', i % 4)])

                def stage3(i):
                    kb, u, amin, mp = tiles[i]
                    Et = E[i % 4]
                    for a in range(amin, 4):
                        p.op('pe', lambda e: e.matmul(bank(st, a)[:, mp * 256:mp * 256 + 129], lhsT=Et[:, (a - amin) * 128:(a - amin + 1) * 128],
                                                      rhs=VH[:, kb, 0:129], start=(kb == 0 and mp == 0), stop=(kb == lastkb[a]),
                                                      skip_group_check=True),
                             reads=[('E', i % 4), ('VH', kb // 16), 'VHones'], writes=[('acc', a)])
                for i in range(nt + 2):
                    if i < nt:
                        stage1(i)
                    if 1 <= i <= nt:
                        stage2(i - 1)
                    if i >= 2:
                        stage3(i - 2)
                for a in range(4):
                    blk = g * 4 + a
                    acc = bank(st, a)
                    ak = ('acc', a)
                    p.op('dve', lambda e: e.reciprocal(out=ep[:, 0:1], in_=acc[:, 128:129]), reads=[ak], writes=[('ep', 0)])
                    p.op('dve', lambda e: e.reciprocal(out=ep[:, 1:2], in_=acc[:, 384:385]), reads=[ak], writes=[('ep', 1)])
                    p.op('dve', lambda e: e.tensor_tensor(out=ep[:, 2:3], in0=ep[:, 1:2], in1=lsm[:, 5:6], op=ALU.mult),
                         reads=[('ep', 1), 'neglam'], writes=[('ep', 2)])
                    p.op('dve', lambda e: e.tensor_scalar(out=d0[:], in0=acc[:, 0:128], scalar1=ep[:, 0:1], scalar2=None, op0=ALU.mult),
                         reads=[ak, ('ep', 0)], writes=['d0'])
                    p.op('dve', lambda e: e.scalar_tensor_tensor(out=dd[:], in0=acc[:, 256:384], scalar=ep[:, 2:3], op0=ALU.mult,
                                                                 in1=d0[:], op1=ALU.add),
                         reads=[ak, ('ep', 2), 'd0'], writes=['dd'])
                    p.op('act', lambda e: e.activation(out=junk[:], in_=dd[:], func=AF.Square, accum_out=ep[:, 3:4]),
                         reads=['dd'], writes=['junk', ('ep', 3)])
                    p.op('act', lambda e: e.activation(out=ep[:, 4:5], in_=ep[:, 3:4], func=AF.Ln, bias=st.eps[:], scale=1.0 / 128.0),
                         reads=[('ep', 3), 'eps'], writes=[('ep', 4)])
                    p.op('act', lambda e: e.activation(out=ep[:, 5:6], in_=ep[:, 4:5], func=AF.Exp, scale=-0.5),
                         reads=[('ep', 4)], writes=[('ep', 5)])
                    p.op('dve', lambda e: e.scalar_tensor_tensor(out=st.OALL[:, blk, h * 128:(h + 1) * 128], in0=dd[:], scalar=ep[:, 5:6],
                                                                 op0=ALU.mult, in1=gsc[:], op1=ALU.mult),
                         reads=['dd', ('ep', 5), 'gsc'], writes=[('OALL', blk)])
        p.barrier()
        p.es = old


def sb_q_projection(p, st, wq_dram):
    with ExitStack() as es2:
        old = p.es
        p.es = es2
        W0 = p.sb('W0q', [128, 8, 1024], BF16)
        k0 = load_w_bf16(p, W0, wq_dram, 'W0q')
        for tcn in range(4):
            sl = slice(tcn * 512, (tcn + 1) * 512)
            for hc in range(8):
                bk = 4 + hc % 2
                for dc in range(8):
                    p.op('pe', lambda e: e.matmul(bank(st, bk), lhsT=W0[:, dc, hc * 128:(hc + 1) * 128], rhs=st.XT[:, dc, sl],
                                                  start=(dc == 0), stop=(dc == 7)),
                         reads=k0 + xt_keys(range(tcn * 4, tcn * 4 + 4)), writes=[('bank', bk)])
                p.op('act', lambda e: e.activation(out=st.QT[:, hc, sl], in_=bank(st, bk), func=AF.Copy, scale=0.125),
                     reads=[('bank', bk)], writes=[('QT', hc, tcn)])
        p.barrier()
        p.es = old


def kv_projection(p, st, wkv_dram, kT_out, v_out):
    with ExitStack() as es2:
        old = p.es
        p.es = es2
        W0 = p.sb('Wk', [128, 8, 1024], BF16)
        W1 = p.sb('Wv', [128, 8, 1024], BF16)
        ko = [p.sb('ko%d' % i, [128, 512], BF16) for i in range(2)]
        k0 = load_w_bf16(p, W0, wkv_dram, 'Wk', col0=0)
        k1 = load_w_bf16(p, W1, wkv_dram, 'Wv', col0=1024)
        cnt = 0
        for tcn in range(4):
            sl = slice(tcn * 512, (tcn + 1) * 512)
            xk = xt_keys(range(tcn * 4, tcn * 4 + 4))
            for hc in range(8):
                bk = 4 + cnt % 2
                t = ko[cnt % 2]
                for dc in range(8):
                    p.op('pe', lambda e: e.matmul(bank(st, bk), lhsT=W0[:, dc, hc * 128:(hc + 1) * 128], rhs=st.XT[:, dc, sl],
                                                  start=(dc == 0), stop=(dc == 7)), reads=k0 + xk, writes=[('bank', bk)])
                p.op('act', lambda e: e.copy(out=t[:], in_=bank(st, bk)), reads=[('bank', bk)], writes=[('ko', cnt % 2)])
                p.dma('sp', kT_out[hc // 2][(hc % 2) * 128:(hc % 2 + 1) * 128, sl], t[:], reads=[('ko', cnt % 2)], writes=[('kTo', tcn, hc)])
                cnt += 1
            for tb in range(4):
                blk = tcn * 4 + tb
                for half in range(2):
                    bk = 4 + cnt % 2
                    t = ko[cnt % 2]
                    for dc in range(8):
                        p.op('pe', lambda e: e.matmul(bank(st, bk), lhsT=st.XT[:, dc, blk * 128:(blk + 1) * 128],
                                                      rhs=W1[:, dc, half * 512:(half + 1) * 512], start=(dc == 0), stop=(dc == 7)),
                             reads=k1 + xt_keys([blk]), writes=[('bank', bk)])
                    p.op('act', lambda e: e.copy(out=t[:], in_=bank(st, bk)), reads=[('bank', bk)], writes=[('ko', cnt % 2)])
                    p.dma('sp', v_out[blk // 4][(blk % 4) * 128:(blk % 4 + 1) * 128, half * 512:(half + 1) * 512], t[:], reads=[('ko', cnt % 2)],
                          writes=[('vo', blk, half)])
                    cnt += 1
        p.barrier()
        p.es = old
    return [('kTo', t, h) for t in range(4) for h in range(8)] + [('vo', b, h) for b in range(NB) for h in range(2)]


def sidx(kb):
    return (kb % 4) * 16 + kb // 4


def attn_sb(p, st, d, KTg, Vg, kvkeys):
    with ExitStack() as es2:
        old = p.es
        p.es = es2
        KT = p.sb('KTp', [128, S], BF16)
        VP = p.sb('VPp', [128, 64, 128], BF16)
        MK = p.sb('MKs', [128, 16, 512], BF16)
        NR = 4
        EZ = [p.sb('EZ0', [128, 512], F32)] * NR
        LP = [p.sb('LP%d' % i, [128, 512], F32) for i in range(NR)]
        WT = [p.sb('WT%d' % i, [128, 512], F32) for i in range(NR)]
        LM = [p.sb('LM%d' % i, [128, 512], BF16) for i in range(NR)]
        AT = [p.sb('AT%d' % i, [128, 512], BF16) for i in range(NR)]
        TRI = p.sb('TRI', [128, 128], BF16)
        ONES = p.sb('ONES', [128, 128], BF16)
        trif = p.sb('trif', [128, 128], F32)
        QZ1 = p.sb('QZs', [128, TQ], BF16)
        QZ = [QZ1, QZ1]
        p.dma('sp', MK[:], d['mask'].rearrange("u k q -> k u q"), writes=['MKs'])
        p.op('pool', lambda e: e.memset(trif[:], 1.0), writes=['trif'])
        p.op('pool', lambda e: e.affine_select(out=trif[:], in_=trif[:], pattern=[[-1, 128]], compare_op=ALU.is_ge, fill=0.0,
                                               base=-1, channel_multiplier=1), reads=['trif'], writes=['trif'])
        p.op('dve', lambda e: e.tensor_scalar(out=TRI[:], in0=trif[:], scalar1=-1.0, scalar2=None, op0=ALU.mult), reads=['trif'], writes=['TRI'])
        p.op('pool', lambda e: e.memset(ONES[:], 1.0), writes=['ONES'])
        for hp in range(8):
            for q4 in range(4):
                r0 = q4 * 256 + (hp % 2) * 128
                p.dma('sp', KT[:, q4 * 2048:(q4 + 1) * 2048], KTg[hp // 2][r0:r0 + 128, :],
                      reads=kvkeys, writes=[('KTp', q4)])
                for i in range(4):
                    vsrc = Vg[i][q4 * 512:(q4 + 1) * 512, hp * 128:(hp + 1) * 128].rearrange("(b p) n -> p b n", p=128)
                    p.dma('sp', VP[:, q4 * 16 + 4 * i:q4 * 16 + 4 * i + 4, :], vsrc, reads=kvkeys, writes=[('VPp', q4, i)])
            for hh in range(2):
                head = hp * 2 + hh
                rows = slice(hh * 64, (hh + 1) * 64)
                orow = slice((1 - hh) * 64, (2 - hh) * 64)
                p.op('pool', lambda e: e.memset(QZ1[orow, :], 0.0), writes=[('QZ', 0)])
                p.op('pool', lambda e: e.tensor_copy(out=QZ1[rows, :], in_=st.QT[rows, hp, :]),
                     reads=[('QT', hp, t) for t in range(4)] + [('QZ', 0)], writes=[('QZ', 0)])
                for g in range(4):
                    tiles = group_tiles(g, True)
                    nt = len(tiles)
                    CB = bank(st, 7)
                    p.op('dve', lambda e: e.memset(CB, 0.0), writes=[('bank', 7)])
                    zs = {}

                    def s1(i):
                        kb, u, amin = tiles[i]
                        n = 512 - 128 * amin
                        zb = 1 + (i % 6)
                        zs[i] = zb
                        p.op('pe', lambda e: e.matmul(bank(st, zb)[:, 0:n], lhsT=KT[:, sidx(kb) * 128:(sidx(kb) + 1) * 128],
                                                      rhs=QZ[hh][:, g * 512 + amin * 128:(g + 1) * 512], start=True, stop=False),
                             reads=[('KTp', sidx(kb) // 16), ('QZ', 0)], writes=[('bank', zb)])

                    def s2(i):
                        kb, u, amin = tiles[i]
                        n = 512 - 128 * amin
                        zb = zs[i]
                        r = i % NR
                        p.op('act', lambda e: e.activation(out=EZ[r][:, 0:n], in_=bank(st, zb)[:, 0:n], func=AF.Exp),
                             reads=[('bank', zb)], writes=[('EZ', 0)])
                        p.op('act', lambda e: e.activation(out=LP[r][:, 0:n], in_=EZ[r][:, 0:n], func=AF.Ln, bias=st.one[:], scale=1.0),
                             reads=[('EZ', 0), 'one'], writes=[('LP', r)])
                        if u is not None:
                            p.op('dve', lambda e: e.tensor_tensor(out=LM[r][:, 0:n], in0=LP[r][:, 0:n], in1=MK[:, u, amin * 128:512], op=ALU.mult),
                                 reads=[('LP', r), 'MKs'], writes=[('LM', r)])
                        else:
                            p.op('dve', lambda e: e.tensor_copy(out=LM[r][:, 0:n], in_=LP[r][:, 0:n]), reads=[('LP', r)], writes=[('LM', r)])

                    def s3(i):
                        kb, u, amin = tiles[i]
                        n = 512 - 128 * amin
                        c0 = amin * 128
                        r = i % NR
                        zb = zs[i]
                        p.op('pe', lambda e: e.matmul(bank(st, zb)[:, 0:n], lhsT=TRI[:], rhs=LM[r][:, 0:n], start=False, stop=True),
                             reads=['TRI', ('LM', r)], writes=[('bank', zb)])
                        p.op('dve', lambda e: e.tensor_tensor(out=WT[r][:, 0:n], in0=bank(st, zb)[:, 0:n], in1=LP[r][:, 0:n], op=ALU.subtract),
                             reads=[('bank', zb), ('LP', r)], writes=[('WT', r)])
                        if i > 0:
                            p.op('dve', lambda e: e.tensor_tensor(out=WT[r][:, 0:n], in0=WT[r][:, 0:n], in1=CB[:, c0:512], op=ALU.subtract),
                                 reads=[('WT', r), ('bank', 7)], writes=[('WT', r)])

                    def s3c(i):
                        kb, u, amin = tiles[i]
                        n = 512 - 128 * amin
                        c0 = amin * 128
                        r = i % NR
                        p.op('pe', lambda e: e.matmul(CB[:, c0:512], lhsT=ONES[:], rhs=LM[r][:, 0:n], start=(i == 0), stop=(i == nt - 1),
                                                      skip_group_check=True),
                             reads=['ONES', ('LM', r)], writes=[('bank', 7)])

                    def s4(i):
                        kb, u, amin = tiles[i]
                        n = 512 - 128 * amin
                        r = i % NR
                        p.op('act', lambda e: e.activation(out=AT[r][:, 0:n], in_=WT[r][:, 0:n], func=AF.Exp),
                             reads=[('WT', r)], writes=[('AT', r)])
                        if u is not None:
                            p.op('dve', lambda e: e.tensor_tensor(out=AT[r][:, 0:n], in0=AT[r][:, 0:n], in1=MK[:, u, amin * 128:512], op=ALU.mult),
                                 reads=[('AT', r), 'MKs'], writes=[('AT', r)])

                    def s5(i):
                        kb, u, amin = tiles[i]
                        r = i % NR
                        for a in range(amin, 4):
                            p.op('pe', lambda e: e.matmul(bank(st, 0)[:, a * 64:(a + 1) * 64], lhsT=AT[r][:, (a - amin) * 128:(a - amin + 1) * 128],
                                                          rhs=VP[:, sidx(kb), hh * 64:(hh + 1) * 64], start=(i == 0 and a == 3), stop=(kb == 0),
                                                          skip_group_check=True),
                                 reads=[('AT', r), ('VPp', sidx(kb) // 16, (sidx(kb) % 16) // 4)], writes=[('bank', 0)])
                    for j in range(nt + 5):
                        if j < nt:
                            s1(j)
                        if 0 <= j - 1 < nt:
                            s2(j - 1)
                        if 0 <= j - 3 < nt:
                            s3c(j - 3)
                        if 0 <= j - 2 < nt:
                            s3(j - 2)
                        if 0 <= j - 3 < nt:
                            s4(j - 3)
                        if 0 <= j - 4 < nt:
                            s5(j - 4)
                    for a in range(4):
                        blk = g * 4 + a
                        p.op('act', lambda e: e.copy(out=st.OALL[:, blk, head * 64:(head + 1) * 64], in_=bank(st, 0)[:, a * 64:(a + 1) * 64]),
                             reads=[('bank', 0)], writes=[('OALL', blk)])
        p.barrier()
        p.es = old


def _din(nc, name, shape, dt=F32):
    return nc.dram_tensor(name, list(shape), dt, kind="ExternalInput").ap()


def _dout(nc, name, shape, dt=F32):
    return nc.dram_tensor(name, list(shape), dt, kind="ExternalOutput").ap()


def _dscr(nc, name, shape, dt):
    return nc.dram_tensor(name, list(shape), dt, kind="Internal").ap()


PEER_IN = [('wpq', (1024, 1024)), ('subk', (128, 8, 128)), ('uT', (1024, NEXP)), ('v', (NEXP, 1024)),
           ('lng0', (1, D)), ('lnb0', (1, D)), ('lng1', (1, D)), ('lnb1', (1, D)), ('w_o', (1024, 1024))]


def build_A(stage=99):
    nc = bass.Bass("TRN2", target_bir_lowering=False)
    dbg = stage < 99
    _scr = _dout if dbg else _dscr
    d = {}
    for nm, shp in [('x_own', (TQ, D)), ('xT', (D, S)), ('pos_q', (1, TQ)), ('pos_k', (1, S)), ('ropec', (128, 3)),
                    ('w_q', (D, D)), ('w_qr', (D, D)), ('w_k', (D, D)), ('w_kr', (D, D)), ('w_v', (D, D)),
                    ('lam', (1, 256)), ('subg', (1, 128)), ('w_kv', (D, 2 * D))] + PEER_IN:
        if stage <= 3 and nm in ('uT', 'v'):
            d[nm] = _dscr(nc, nm, shp, F32)
            continue
        d[nm] = _din(nc, nm, shp)
    d['mask'] = _din(nc, 'mask', (16, 128, 512), BF16)
    x1 = _dout(nc, 'x1', (TQ, D))
    kT = _dout(nc, 'kT', (D, TQ))
    vv = _dout(nc, 'vv', (TQ, D))
    KTs = _scr(nc, 'KTs', (D, S), BF16)
    Vs = _scr(nc, 'Vs', (S, D), BF16)
    Xmid = _scr(nc, 'Xmid', (TQ, D), F32)
    uT_bf = _dscr(nc, 'uT_bf', (D, NEXP), BF16)
    v_bf = _dscr(nc, 'v_bf', (NEXP, D), BF16)
    with ExitStack() as es:
        p = Prog(nc, es)
        st = St()
        setup_common(p, st)
        st.ropec = p.sb('ropec', [128, 3], F32)
        p.dma('sp', st.ropec[:], d['ropec'], writes=['ropec'])
        make_xt(p, st, d['x_own'], 'x_own')
        with ExitStack() as esa:
            p.es = esa
            st.QT = p.sb('QT', [128, 8, TQ], BF16)
            diff_projections(p, st, d, KTs, Vs)
            if stage == 1:
                dq = _dout(nc, 'dbg_QT', (128, 8, TQ), BF16)
                dx = _dout(nc, 'dbg_XT', (128, 8, TQ), BF16)
                p.dma('sp', dq, st.QT[:], reads=[('QT', h, t) for h in range(8) for t in range(4)], writes=['dq'])
                p.dma('sp', dx, st.XT[:], reads=xt_keys(range(NB)), writes=['dx'])
                p.wait_all('sp', ['dq', 'dx'] + [('KTs', t) for t in range(16)] + [('Vs', t) for t in range(16)])
                return nc
            st.OALL = p.sb('OALL', [128, NB, D], BF16)
            st.OT = p.sb('OT', [128, 8, 128], BF16)
            convkeys = peer_convert(p, d['uT'], d['v'], uT_bf, v_bf, 'c0')
            attn_diff(p, st, d, KTs, Vs)
            if stage == 2:
                do = _dout(nc, 'dbg_OALL', (128, NB, D), BF16)
                p.dma('sp', do, st.OALL[:], reads=[('OALL', b) for b in range(NB)], writes=['do'])
                p.wait_all('sp', ['do'])
                return nc
            WO = p.sb('WO', [128, 8, 1024], BF16)
            wok = load_w_bf16(p, WO, d['w_o'], 'WO')
            load_ln(p, st, d['lng0'], d['lnb0'])
            post_attn(p, st, WO, wok, d['x_own'], 'x_own', Xmid, 'Xmid')
            p.barrier()
            if stage == 3:
                dx = _dout(nc, 'dbg_XT', (128, 8, TQ), BF16)
                p.dma('sp', dx, st.XT[:], reads=xt_keys(range(NB)), writes=['dx'])
                p.wait_all('sp', ['dx'] + [('Xmid', b) for b in range(NB)])
                return nc
            p.es = es
        with ExitStack() as esb:
            p.es = esb
            alloc_peer(p, st)
            peer_layer(p, st, d['wpq'], d['subk'], uT_bf, v_bf, convkeys, d['lng1'], d['lnb1'], Xmid, 'Xmid', x1, 'x1', True)
            p.barrier()
            p.es = es
        if stage == 4:
            p.wait_all('sp', [('x1', b) for b in range(_DBG.get('peer_blocks', NB))])
            return nc
        okeys = kv_projection(p, st, d['w_kv'], kT, vv)
        p.wait_all('sp', okeys + [('x1', b) for b in range(NB)])
        print("build_A ops", p.nops, "waits", p.nwaits, flush=True)
    return nc


def build_F():
    nc = bass.Bass("TRN2", target_bir_lowering=False)
    d = {}
    for nm, shp in [('x_own', (TQ, D)), ('xT', (D, S)), ('pos_q', (1, TQ)), ('pos_k', (1, S)), ('ropec', (128, 3)),
                    ('w_q', (D, D)), ('w_qr', (D, D)), ('w_k', (D, D)), ('w_kr', (D, D)), ('w_v', (D, D)),
                    ('lam', (1, 256)), ('subg', (1, 128)), ('w_kv', (D, 2 * D)), ('w_qb', (D, D))]:
        d[nm] = _din(nc, nm, shp)
    d0, d1 = {}, {}
    for nm, shp in PEER_IN:
        d0[nm] = _din(nc, nm + '_0', shp)
        d1[nm] = _din(nc, nm + '_1', shp)
    d['mask'] = _din(nc, 'mask', (16, 128, 512), BF16)
    dsb = {'mask': _din(nc, 'mask_sb', (16, 128, 512), BF16)}
    out = _dout(nc, 'out', (TQ, D))
    KTs = _dscr(nc, 'KTs', (D, S), BF16)
    Vs = _dscr(nc, 'Vs', (S, D), BF16)
    Xmid = _dscr(nc, 'Xmid', (TQ, D), F32)
    X1 = _dscr(nc, 'X1', (TQ, D), F32)
    uT_bf = [_dscr(nc, 'uT_bf%d' % l, (D, NEXP), BF16) for l in range(2)]
    v_bf = [_dscr(nc, 'v_bf%d' % l, (NEXP, D), BF16) for l in range(2)]
    kT_loc = [_dscr(nc, 'kT_loc%d' % i, (256, TQ), BF16) for i in range(4)]
    vv_loc = [_dscr(nc, 'vv_loc%d' % i, (512, D), BF16) for i in range(4)]
    KTg = [_dscr(nc, 'KTg%d' % i, (4 * 256, TQ), BF16) for i in range(4)]
    Vg = [_dscr(nc, 'Vg%d' % i, (4 * 512, D), BF16) for i in range(4)]
    groups = [[0, 1, 2, 3], [4, 5, 6, 7]]
    with ExitStack() as es:
        p = Prog(nc, es)
        st = St()
        setup_common(p, st)
        st.ropec = p.sb('ropec', [128, 3], F32)
        p.dma('sp', st.ropec[:], d['ropec'], writes=['ropec'])
        make_xt(p, st, d['x_own'], 'x_own')
        with ExitStack() as esa:
            p.es = esa
            st.QT = p.sb('QT', [128, 8, TQ], BF16)
            diff_projections(p, st, d, KTs, Vs)
            st.OALL = p.sb('OALL', [128, NB, D], BF16)
            st.OT = p.sb('OT', [128, 8, 128], BF16)
            conv0 = peer_convert(p, d0['uT'], d0['v'], uT_bf[0], v_bf[0], 'c0')
            attn_diff(p, st, d, KTs, Vs)
            conv1 = peer_convert(p, d1['uT'], d1['v'], uT_bf[1], v_bf[1], 'c1')
            WO = p.sb('WO', [128, 8, 1024], BF16)
            wok = load_w_bf16(p, WO, d0['w_o'], 'WO')
            load_ln(p, st, d0['lng0'], d0['lnb0'])
            post_attn(p, st, WO, wok, d['x_own'], 'x_own', Xmid, 'Xmid')
            p.barrier()
            p.es = es
        with ExitStack() as esb:
            p.es = esb
            alloc_peer(p, st)
            peer_layer(p, st, d0['wpq'], d0['subk'], uT_bf[0], v_bf[0], conv0, d0['lng1'], d0['lnb1'], Xmid, 'Xmid', X1, 'X1', True)
            p.barrier()
            p.es = es
        okeys = kv_projection(p, st, d['w_kv'], kT_loc, vv_loc)
        p.barrier()
        gkeys = []
        for i in range(4):
            p.collective("AllGather", groups, kT_loc[i], KTg[i], reads=okeys, writes=[('KTg', i)])
            p.collective("AllGather", groups, vv_loc[i], Vg[i], reads=okeys, writes=[('Vg', i)])
            gkeys += [('KTg', i), ('Vg', i)]
        p.barrier()
        with ExitStack() as esa:
            p.es = esa
            st.QT = p.sb('QT1', [128, 8, TQ], BF16)
            sb_q_projection(p, st, d['w_qb'])
            st.OALL = p.sb('OALL1', [128, NB, D], BF16)
            st.OT = p.sb('OT1', [128, 8, 128], BF16)
            attn_sb(p, st, dsb, KTg, Vg, gkeys)
            WO = p.sb('WO1', [128, 8, 1024], BF16)
            wok = load_w_bf16(p, WO, d1['w_o'], 'WO1')
            load_ln(p, st, d1['lng0'], d1['lnb0'])
            post_attn(p, st, WO, wok, X1, 'X1', Xmid, 'Xmid')
            p.barrier()
            p.es = es
        with ExitStack() as esb:
            p.es = esb
            alloc_peer(p, st)
            peer_layer(p, st, d1['wpq'], d1['subk'], uT_bf[1], v_bf[1], conv1, d1['lng1'], d1['lnb1'], Xmid, 'Xmid', out, 'out', False)
            p.barrier()
            p.es = es
        p.wait_all('sp', [('out', b) for b in range(NB)])
        print("build_F ops", p.nops, "waits", p.nwaits, flush=True)
    return nc


def _own_index(j):
    m = np.arange(NB)[:, None]
    r = np.arange(128)[None, :]
    return ((4 * m + j) * 128 + r).reshape(-1)


def _masks(j, kind):
    k = np.arange(128)[:, None]
    q = np.arange(128)[None, :]
    if kind == 'diff':
        diag = (k // 64) <= (q // 64)
    else:
        diag = k < q
    m = np.zeros((16, 128, 4, 128), np.float32)
    for u in range(16):
        for a in range(4):
            if u < 4 * a + j:
                m[u, :, a, :] = 1.0
            elif u == 4 * a + j:
                m[u, :, a, :] = diag
    return m.reshape(16, 128, 512).astype(ml_dtypes.bfloat16)


def _peer_inputs(layer, ln_g, ln_b, peer_w_q, peer_sub_keys, peer_u, peer_v, w_o):
    sk = np.ascontiguousarray(np.transpose(peer_sub_keys[layer], (1, 3, 0, 2)).reshape(128, 8, 128))
    return {
        'wpq': np.ascontiguousarray(peer_w_q[layer]),
        'subk': sk,
        'uT': np.ascontiguousarray(peer_u[layer].T),
        'v': np.ascontiguousarray(peer_v[layer]),
        'lng0': np.ascontiguousarray(ln_g[layer, 0][None, :]), 'lnb0': np.ascontiguousarray(ln_b[layer, 0][None, :]),
        'lng1': np.ascontiguousarray(ln_g[layer, 1][None, :]), 'lnb1': np.ascontiguousarray(ln_b[layer, 1][None, :]),
        'w_o': np.ascontiguousarray(w_o),
    }


def kernel(x, ln_g, ln_b, w_qkv_a, w_o_a, lambda_qk_a, subln_g_a, w_kv_b, w_q_b, w_o_b,
           peer_w_q, peer_sub_keys, peer_u, peer_v):
    f = lambda a: np.asarray(a, dtype=np.float32)
    x, ln_g, ln_b, w_qkv_a, w_o_a, lambda_qk_a, subln_g_a, w_kv_b, w_q_b, w_o_b, peer_w_q, peer_sub_keys, peer_u, peer_v = map(
        f, (x, ln_g, ln_b, w_qkv_a, w_o_a, lambda_qk_a, subln_g_a, w_kv_b, w_q_b, w_o_b, peer_w_q, peer_sub_keys, peer_u, peer_v))
    B = x.shape[0]
    perm = np.arange(D).reshape(16, 2, 32)[:, ::-1, :].reshape(-1)
    wq = np.ascontiguousarray(w_qkv_a[0][:, 0:D])
    wk = np.ascontiguousarray(w_qkv_a[0][:, D:2 * D])
    wv = np.ascontiguousarray(w_qkv_a[0][:, 2 * D:3 * D])
    wqr = np.ascontiguousarray(wq[:, perm])
    wkr = np.ascontiguousarray(wk[:, perm])
    dd_ = np.arange(128) % 64
    invf = (10000.0 ** (-(dd_ % 32).astype(np.float64) / 32.0)) / (2.0 * np.pi)
    sgn = np.where(dd_ < 32, -1.0, 1.0)
    ropec = np.stack([invf, sgn * TWO_PI_SAFE, np.full(128, TWO_PI_SAFE)], axis=1).astype(np.float32)
    pos_k = np.arange(S, dtype=np.float32)[None, :]
    xT = [np.ascontiguousarray(x[b].T) for b in range(B)]
    pa = _peer_inputs(0, ln_g, ln_b, peer_w_q, peer_sub_keys, peer_u, peer_v, w_o_a[0])
    common = {'pos_k': pos_k, 'ropec': ropec, 'w_q': wq, 'w_qr': wqr, 'w_k': wk, 'w_kr': wkr, 'w_v': wv,
              'lam': np.ascontiguousarray(lambda_qk_a[0].reshape(1, 256)), 'subg': np.ascontiguousarray(subln_g_a[0][None, :]),
              'w_kv': np.ascontiguousarray(w_kv_b)}
    common.update(pa)
    in_maps = []
    idxs = []
    for c in range(8):
        b, j = c // 4, c % 4
        idx = _own_index(j)
        idxs.append(idx)
        m = dict(common)
        m['x_own'] = np.ascontiguousarray(x[b, idx, :])
        m['xT'] = xT[b]
        m['pos_q'] = idx.astype(np.float32)[None, :]
        m['mask'] = _masks(j, 'diff')
        in_maps.append(m)
    if _DBG.get('stageA') is not None:
        if _DBG['stageA'] <= 3:
            in_maps = [{k: v for k, v in m.items() if k not in ('uT', 'v')} for m in in_maps]
        return run_bass_kernel_spmd(build_A(_DBG['stageA']), in_maps, core_ids=list(range(8))).results, in_maps
    pb = _peer_inputs(1, ln_g, ln_b, peer_w_q, peer_sub_keys, peer_u, peer_v, w_o_b[0])
    for c in range(8):
        m = in_maps[c]
        for k in [nm for nm, _ in PEER_IN]:
            m[k + '_0'] = m.pop(k)
            m[k + '_1'] = pb[k]
        m['w_qb'] = np.ascontiguousarray(w_q_b[0])
        m['mask_sb'] = _masks(c % 4, 'sb')
    ncF = build_F()
    resB = run_bass_kernel_spmd(ncF, in_maps, core_ids=list(range(8))).results
    out = np.zeros((B, S, D), np.float32)
    for c in range(8):
        out[c // 4, idxs[c], :] = np.asarray(resB[c]['out'], dtype=np.float32)
    return out


def build_P(real_tables):
    nc = bass.Bass("TRN2", target_bir_lowering=False)
    d = {}
    for nm, shp in [('x_own', (TQ, D))] + PEER_IN:
        if not real_tables and nm in ('uT', 'v'):
            d[nm] = _dscr(nc, nm, shp, F32)
            continue
        d[nm] = _din(nc, nm, shp)
    x1 = _dout(nc, 'x1', (TQ, D))
    dbg = _dout(nc, 'dbg', (128, 16), F32)
    dga = _dout(nc, 'dga', (128, 4096), BF16)
    uT_bf = _dscr(nc, 'uT_bf', (D, NEXP), BF16)
    v_bf = _dscr(nc, 'v_bf', (NEXP, D), BF16)
    with ExitStack() as es:
        p = Prog(nc, es)
        st = St()
        setup_common(p, st)
        make_xt(p, st, d['x_own'], 'x_own')
        convkeys = peer_convert(p, d['uT'], d['v'], uT_bf, v_bf, 'c0')
        alloc_peer(p, st)
        peer_layer(p, st, d['wpq'], d['subk'], uT_bf, v_bf, convkeys, d['lng1'], d['lnb1'], d['x_own'], 'x_own', x1, 'x1', True)
        p.barrier()
        p.dma('sp', dbg[:, 0:8], st.peer['BIAS'][:], writes=['dbg0'])
        p.dma('sp', dbg[:, 8:16], st.peer['GTH'][:], writes=['dbg1'])
        p.dma('sp', dga, st.peer['GA'][1][:], writes=['dga'])
        p.barrier()
        print("build_P ops", p.nops, "waits", p.nwaits, flush=True)
    return nc
```
